# Optimizing a Trainium2 kernel written in Bass

```python
import jax, jax.numpy as jnp
from jax import lax
import numpy as np

D_MODEL = 1024
BATCH = 4
SEQ = 4096
DEPTH = 1
DEC_BATCH = 128
DEC_SEQ = 4
PAST_LEN = 2048
PAGE_SIZE = 128

HA = 12
DA = 64
EA = HA * DA
PATTERNS = ((128, 1), (512, 4), (2048, 16))
WINDOW_MAX = 2048
QB = 128
EB = 768
CONV_W = 31
HC = 4
DC = 128
EC = HC * DC
N_MEM = 256
N_BUCKETS = 32
MAX_DIST = 2048
EPS = 1e-6
NEG = -1e30
SPLITS = (EA, 2 * EA, 3 * EA, 4 * EA, 4 * EA + 2 * EB, 4 * EA + 3 * EB,
          4 * EA + 3 * EB + EC, 4 * EA + 3 * EB + 2 * EC)
IN_COLS = 4 * EA + 3 * EB + 2 * EC + 3 * D_MODEL

kernel_name = 'hybrid_dilated_conv_memory_decoder_step'


def _rmsnorm(x, g):
    x32 = x.astype(jnp.float32)
    y = x32 * lax.rsqrt(jnp.mean(x32 * x32, axis=-1, keepdims=True) + EPS)
    return (y * g.astype(jnp.float32)).astype(x.dtype)


def _layernorm(x, g, b):
    x32 = x.astype(jnp.float32)
    mu = jnp.mean(x32, axis=-1, keepdims=True)
    var = jnp.mean(jnp.square(x32 - mu), axis=-1, keepdims=True)
    y = (x32 - mu) * lax.rsqrt(var + EPS)
    return (y * g.astype(jnp.float32) + b.astype(jnp.float32)).astype(x.dtype)


def _t5_buckets(n):
    exact = N_BUCKETS // 2
    nf = np.maximum(n, 1).astype(np.float32)
    scale = np.float32(N_BUCKETS - exact) / np.log(np.float32(MAX_DIST) / np.float32(exact))
    large = exact + (np.log(nf / np.float32(exact)) * scale).astype(np.int32)
    large = np.minimum(large, N_BUCKETS - 1)
    return np.where(n < exact, n, large).astype(np.int32)


def _pattern_bias(rel_bias, d, nw):
    idx = _t5_buckets(np.arange(nw + 1, dtype=np.int32) * d)
    return rel_bias[jnp.asarray(idx)].T


def _dilated_band(q, k, v, bias, d, nw):
    B, S, H, Dh = q.shape
    L = S // d
    N = B * d

    def to_res(t):
        return t.reshape(B, L, d, H, Dh).transpose(0, 2, 1, 3, 4).reshape(N, L, H, Dh)

    qr, kr, vr = to_res(q), to_res(k), to_res(v)
    nb = -(-L // QB)
    Lp = nb * QB
    qb = jnp.pad(qr, ((0, 0), (0, Lp - L), (0, 0), (0, 0))).reshape(N, nb, QB, H, Dh)
    kp = jnp.pad(kr, ((0, 0), (QB, Lp - L), (0, 0), (0, 0))).reshape(N, nb + 1, QB, H, Dh)
    vp = jnp.pad(vr, ((0, 0), (QB, Lp - L), (0, 0), (0, 0))).reshape(N, nb + 1, QB, H, Dh)
    kb = jnp.concatenate([kp[:, :-1], kp[:, 1:]], axis=2)
    vb = jnp.concatenate([vp[:, :-1], vp[:, 1:]], axis=2)
    iq = jnp.arange(QB)[:, None]
    ik = jnp.arange(2 * QB)[None, :]
    dist = iq - ik + QB
    band = (dist >= 0) & (dist <= nw)
    blk = jnp.arange(nb)[:, None, None]
    valid = band[None] & ((blk > 0) | (ik >= QB)[None])
    s = jnp.einsum('nbqhd,nbkhd->nbhqk', qb, kb).astype(jnp.float32)
    s = s + bias[:, jnp.clip(dist, 0, nw)].astype(jnp.float32)
    s = jnp.where(valid[None, :, None], s, NEG)
    m = jnp.max(s, axis=-1, keepdims=True)
    p = jnp.exp(s - m)
    l = jnp.sum(p, axis=-1, keepdims=True)
    o = jnp.einsum('nbhqk,nbkhd->nbqhd', (p / l).astype(v.dtype), vb)
    lse = (m + jnp.log(l))[..., 0]
    o = o.reshape(N, Lp, H, Dh)[:, :L]
    lse = lse.transpose(0, 1, 3, 2).reshape(N, Lp, H)[:, :L]
    o = o.reshape(B, d, L, H, Dh).transpose(0, 2, 1, 3, 4).reshape(B, S, H, Dh)
    lse = lse.reshape(B, d, L, H).transpose(0, 2, 1, 3).reshape(B, S, H)
    return o, lse


def _dilated_gather(q, k_all, v_all, bias, d, nw, wb, past):
    T = q.shape[1]
    i = jnp.arange(T)[:, None]
    j = jnp.arange(nw + 1)[None, :]
    kpos = past + i - j * d
    idx = wb + i - j * d
    valid = (kpos >= 0) & (idx >= 0)
    idx = jnp.maximum(idx, 0)
    kg = k_all[:, idx]
    vg = v_all[:, idx]
    s = jnp.einsum('bthd,btjhd->bthj', q, kg).astype(jnp.float32)
    s = s + bias.astype(jnp.float32)
    s = jnp.where(valid[None, :, None, :], s, NEG)
    m = jnp.max(s, axis=-1, keepdims=True)
    p = jnp.exp(s - m)
    l = jnp.sum(p, axis=-1, keepdims=True)
    o = jnp.einsum('bthj,btjhd->bthd', (p / l).astype(v_all.dtype), vg)
    return o, (m + jnp.log(l))[..., 0]


def _combine(outs, lses):
    w = jax.nn.softmax(jnp.stack(lses), axis=0)
    return jnp.einsum('gbshd,gbsh->bshd', jnp.stack(outs), w.astype(outs[0].dtype))


def _mem_kv(mem, g_mem, w_mem_kv):
    B = mem.shape[0]
    kv = _rmsnorm(mem, g_mem) @ w_mem_kv
    k, v = jnp.split(kv, 2, axis=-1)
    return k.reshape(B, N_MEM, HC, DC), v.reshape(B, N_MEM, HC, DC)


def _layer(x, mem_k, mem_v, hist_k, hist_v, hist_conv, past,
           rel_bias, g_pre, w_in, conv_w, conv_b, ln_g, ln_b,
           w_pa, w_pb, w_pc, w_out, g_post):
    Bx, S, _ = x.shape
    h = _rmsnorm(x, g_pre) @ w_in
    qa, ka, va, za, ub, zb, qc, zc, gt = jnp.split(h, SPLITS, axis=-1)
    qa = qa.reshape(Bx, S, HA, DA) * (DA ** -0.5)
    ka = ka.reshape(Bx, S, HA, DA)
    va = va.reshape(Bx, S, HA, DA)

    outs, lses = [], []
    if hist_k is None:
        for (w, d) in PATTERNS:
            nw = w // d
            o, lse = _dilated_band(qa, ka, va, _pattern_bias(rel_bias, d, nw), d, nw)
            outs.append(o)
            lses.append(lse)
        wbp = min(WINDOW_MAX, S)
        k_rows, v_rows = ka[:, S - wbp:], va[:, S - wbp:]
    else:
        wb = hist_k.shape[1]
        k_all = jnp.concatenate([hist_k, ka], axis=1)
        v_all = jnp.concatenate([hist_v, va], axis=1)
        for (w, d) in PATTERNS:
            nw = w // d
            o, lse = _dilated_gather(qa, k_all, v_all, _pattern_bias(rel_bias, d, nw), d, nw, wb, past)
            outs.append(o)
            lses.append(lse)
        k_rows, v_rows = ka, va
    oa = _combine(outs, lses).reshape(Bx, S, EA)

    glu = ub[..., :EB] * jax.nn.sigmoid(ub[..., EB:])
    if hist_conv is None:
        hist_conv = jnp.zeros((Bx, CONV_W - 1, EB), glu.dtype)
    full = jnp.concatenate([hist_conv, glu], axis=1)
    conv = lax.conv_general_dilated(full, conv_w[:, None, :], window_strides=(1,), padding='VALID',
                                    dimension_numbers=('NWC', 'WIO', 'NWC'),
                                    feature_group_count=EB) + conv_b
    ob = jax.nn.silu(_layernorm(conv, ln_g, ln_b))
    conv_state = full[:, -(CONV_W - 1):]

    qc = qc.reshape(Bx, S, HC, DC) * (DC ** -0.5)
    sc = jnp.einsum('bshd,bmhd->bhsm', qc, mem_k).astype(jnp.float32)
    pc = jax.nn.softmax(sc, axis=-1).astype(mem_v.dtype)
    oc = jnp.einsum('bhsm,bmhd->bshd', pc, mem_v).reshape(Bx, S, EC)

    gates = jax.nn.sigmoid(gt).reshape(Bx, S, 3, D_MODEL)
    mix = (gates[..., 0, :] * ((oa * jax.nn.silu(za)) @ w_pa)
           + gates[..., 1, :] * ((ob * jax.nn.silu(zb)) @ w_pb)
           + gates[..., 2, :] * ((oc * jax.nn.silu(zc)) @ w_pc))
    y = x + _rmsnorm(mix @ w_out, g_post)
    return y, k_rows, v_rows, conv_state


def setup_inputs(seed: int = 0) -> dict:
    key = jax.random.key(seed)
    ks = jax.random.split(key, 24)
    f32 = jnp.float32
    wb = min(WINDOW_MAX, PAST_LEN)

    def nrm(k, shape, scale=1.0):
        return jax.random.normal(k, shape, f32) * scale

    return {
        'x_prompt': nrm(ks[0], (BATCH, SEQ, D_MODEL)),
        'x_sample': nrm(ks[1], (DEC_BATCH, DEC_SEQ, D_MODEL)),
        'mem_prompt': nrm(ks[2], (BATCH, N_MEM, D_MODEL)),
        'cache_k_win': nrm(ks[3], (DEPTH, DEC_BATCH, wb, HA, DA)),
        'cache_v_win': nrm(ks[4], (DEPTH, DEC_BATCH, wb, HA, DA)),
        'state_conv': nrm(ks[5], (DEPTH, DEC_BATCH, CONV_W - 1, EB), 0.5),
        'cache_k_mem': nrm(ks[6], (DEPTH, DEC_BATCH, N_MEM, HC, DC)),
        'cache_v_mem': nrm(ks[7], (DEPTH, DEC_BATCH, N_MEM, HC, DC)),
        'rel_bias': nrm(ks[8], (N_BUCKETS, HA), 0.5),
        'g_pre': 1.0 + nrm(ks[9], (DEPTH, D_MODEL), 0.05),
        'w_in': nrm(ks[10], (DEPTH, D_MODEL, IN_COLS), D_MODEL ** -0.5),
        'g_mem': 1.0 + nrm(ks[11], (DEPTH, D_MODEL), 0.05),
        'w_mem_kv': nrm(ks[12], (DEPTH, D_MODEL, 2 * EC), D_MODEL ** -0.5),
        'conv_w': nrm(ks[13], (DEPTH, CONV_W, EB), CONV_W ** -0.5),
        'conv_b': nrm(ks[14], (DEPTH, EB), 0.02),
        'ln_g': 1.0 + nrm(ks[15], (DEPTH, EB), 0.05),
        'ln_b': nrm(ks[16], (DEPTH, EB), 0.02),
        'w_proj_a': nrm(ks[17], (DEPTH, EA, D_MODEL), EA ** -0.5),
        'w_proj_b': nrm(ks[18], (DEPTH, EB, D_MODEL), EB ** -0.5),
        'w_proj_c': nrm(ks[19], (DEPTH, EC, D_MODEL), EC ** -0.5),
        'w_out': nrm(ks[20], (DEPTH, D_MODEL, D_MODEL), D_MODEL ** -0.5),
        'g_post': 1.0 + nrm(ks[21], (DEPTH, D_MODEL), 0.05),
    }


def reference(x_prompt, x_sample, mem_prompt, cache_k_win, cache_v_win, state_conv,
              cache_k_mem, cache_v_mem, rel_bias, g_pre, w_in, g_mem, w_mem_kv,
              conv_w, conv_b, ln_g, ln_b, w_proj_a, w_proj_b, w_proj_c, w_out, g_post):
    y_p, y_s = x_prompt, x_sample
    kw_p, vw_p, cv_p, km_p, vm_p = [], [], [], [], []
    kw_s, vw_s, cv_s = [], [], []
    for l in range(DEPTH):
        w = (rel_bias, g_pre[l], w_in[l], conv_w[l], conv_b[l], ln_g[l], ln_b[l],
             w_proj_a[l], w_proj_b[l], w_proj_c[l], w_out[l], g_post[l])
        mk, mv = _mem_kv(mem_prompt, g_mem[l], w_mem_kv[l])
        y_p, k_r, v_r, c_r = _layer(y_p, mk, mv, None, None, None, 0, *w)
        kw_p.append(k_r)
        vw_p.append(v_r)
        cv_p.append(c_r)
        km_p.append(mk)
        vm_p.append(mv)
        y_s, k_r, v_r, c_r = _layer(y_s, cache_k_mem[l], cache_v_mem[l], cache_k_win[l],
                                    cache_v_win[l], state_conv[l], PAST_LEN, *w)
        kw_s.append(k_r)
        vw_s.append(v_r)
        cv_s.append(c_r)
    return (y_p, y_s, jnp.stack(kw_p), jnp.stack(vw_p), jnp.stack(cv_p), jnp.stack(km_p),
            jnp.stack(vm_p), jnp.stack(kw_s), jnp.stack(vw_s), jnp.stack(cv_s))
```

```python
import numpy as np
import concourse.bass as bass
import concourse.mybir as mybir
from concourse.bass_utils import run_bass_kernel_spmd
from contextlib import ExitStack

F32 = mybir.dt.float32; BF16 = mybir.dt.bfloat16
AF = mybir.ActivationFunctionType; ALU = mybir.AluOpType; AX = mybir.AxisListType
NEG = -1e30; EPS = 1e-6
T = 2048; TS = 64; NT = T + TS
PATS = (1, 4, 16)


class Buf:
    __slots__ = ('name', 'w', 'r')
    def __init__(self, name='b'):
        self.name = name; self.w = None; self.r = []


class Tl:
    def __init__(self, t, name, d=None):
        self.t = t; self.b = Buf(name); self.d = d


class Sched:
    ENG = ['pe', 'act', 'dve', 'pool', 'sp']
    def __init__(self, nc, es):
        self.nc = nc; self.es = es
        self.ops = {e: [] for e in self.ENG}
        self.cnt = {e: 0 for e in self.ENG}
        self.sems = []
        self.esem = {}
        for e in self.ENG:
            self.esem[e] = self.new_sem('s_' + e)
        self.known = {e: {} for e in self.ENG}
        self.dsems = []
        self.nev = 0
        self.halt = False
    def new_sem(self, name):
        s = self.es.enter_context(self.nc.semaphore(name))
        self.sems.append(s)
        return len(self.sems) - 1
    def dma_sem(self, name):
        d = {'key': self.new_sem(name), 'val': 0}
        self.dsems.append(d)
        return d
    def _waits(self, eng, reads, writes):
        deps = {}
        def add(t):
            if t is None: return
            k, v = t
            if deps.get(k, 0) < v: deps[k] = v
        for b in reads: add(b.w)
        for b in writes:
            add(b.w)
            for t in b.r: add(t)
        waits = []
        kn = self.known[eng]
        for k, v in deps.items():
            if eng == 'pe' and k == self.esem['pe']: continue
            if kn.get(k, 0) >= v: continue
            kn[k] = v
            waits.append((k, v))
        return waits
    def op(self, eng, fn, reads=(), writes=()):
        if self.halt: return None
        waits = self._waits(eng, reads, writes)
        self.cnt[eng] += 1
        tok = (self.esem[eng], self.cnt[eng])
        self.ops[eng].append((waits, fn, ('c', self.esem[eng])))
        for b in reads: b.r.append(tok)
        for b in writes: b.w = tok; b.r = []
        return tok
    def dma(self, q, out_ap, in_ap, dsem, reads=(), writes=()):
        if self.halt: return None
        waits = self._waits(q, reads, writes)
        dsem['val'] += 16
        tok = (dsem['key'], dsem['val'])
        self.ops[q].append((waits, lambda e, o=out_ap, i=in_ap: e.dma_start(out=o, in_=i), ('d', dsem['key'])))
        for b in reads: b.r.append(tok)
        for b in writes: b.w = tok; b.r = []
        return tok
    def ev(self):
        self.nev += 1
        return 'act' if self.nev % 2 else 'dve'
    def barrier(self):
        fin = [(self.esem[e], self.cnt[e]) for e in self.ENG if self.cnt[e] > 0]
        fin += [(d['key'], d['val']) for d in self.dsems if d['val'] > 0]
        for e in self.ENG:
            kn = self.known[e]; w = []
            for k, v in fin:
                if kn.get(k, 0) >= v: continue
                kn[k] = v; w.append((k, v))
            self.ops[e].append((w, None, None))
    def emit(self):
        nc = self.nc
        fin = [(self.esem[e], self.cnt[e]) for e in self.ENG if self.cnt[e] > 0]
        fin += [(d['key'], d['val']) for d in self.dsems if d['val'] > 0]
        def run(eng_obj, name):
            for waits, fn, tok in self.ops[name]:
                for k, v in waits:
                    eng_obj.wait_ge(self.sems[k], v)
                if fn is None: continue
                inst = fn(eng_obj)
                if tok[0] == 'c': inst.then_inc(self.sems[tok[1]], 1)
                else: inst.then_inc(self.sems[tok[1]], 16)
            if name == 'sp':
                for k, v in fin: eng_obj.wait_ge(self.sems[k], v)
        with nc.Block() as block:
            @block.sync
            def _(e): run(e, 'sp')
            @block.tensor
            def _(e): run(e, 'pe')
            @block.scalar
            def _(e): run(e, 'act')
            @block.vector
            def _(e): run(e, 'dve')
            @block.gpsimd
            def _(e): run(e, 'pool')


class StopBuild(Exception):
    pass


class Ring:
    def __init__(self, tiles):
        self.tiles = tiles; self.i = 0
    def next(self):
        t = self.tiles[self.i % len(self.tiles)]; self.i += 1
        return t


def t5_buckets(n):
    NB = 32; exact = 16
    nf = np.maximum(n, 1).astype(np.float32)
    scale = np.float32(NB - exact) / np.log(np.float32(2048) / np.float32(exact))
    large = exact + (np.log(nf / np.float32(exact)) * scale).astype(np.int32)
    large = np.minimum(large, NB - 1)
    return np.where(n < exact, n, large).astype(np.int32)


def prompt_bias_tables():
    k = np.arange(128)[:, None]; c = np.arange(256)[None, :]
    q = c % 128
    dist = np.where(c < 128, q - k + 128, q - k)
    valid = (dist >= 0) & (dist <= 128)
    idx = np.zeros((3, 128, 256), np.int32)
    for gi, d in enumerate(PATS):
        idx[gi] = t5_buckets(np.clip(dist, 0, 128) * d)
    return idx, valid


def sample_bias_tables():
    idx = np.zeros((9, 128, 4), np.int32); add = np.full((9, 128, 4), NEG, np.float32)
    def put(ti, p, t, dist, m):
        idx[ti, p, t] = t5_buckets(np.array([dist]))[0]; add[ti, p, t] = np.log(np.float32(m))
    for tp in range(4):
        for p in range(96):
            put(tp, p, tp, 2048 - 16 * p, 1)
    for a in range(4):
        for p in range(128):
            for t in range(4):
                dist = 512 + t - 128 * a - p
                m = int(dist <= 128) + int(dist % 4 == 0 and dist <= 512) + int(dist % 16 == 0)
                if m > 0: put(4 + a, p, t, dist, m)
    for p in range(4):
        for t in range(4):
            dist = t - p
            if dist >= 0: put(8, p, t, dist, 3 if dist == 0 else 1)
    return idx, add


STAGES = ['norm', 'mem', 'kvtm', 'attn', 'sattn', 'cattn', 'conv', 'merge', 'final']


SQ = 'pool'
VARIANT = ''


def build(stop=None):
    nc = bass.Bass("TRN2", target_bir_lowering=False)
    dI = lambda n, s, dt=F32: nc.dram_tensor(n, list(s), dt, kind="ExternalInput").ap()
    dO = lambda n, s: nc.dram_tensor(n, list(s), F32, kind="ExternalOutput").ap()
    xo = dI("xo", [T, 1024]); xh = dI("xh", [T, 1024]); xs = dI("xs", [TS, 1024]); mem = dI("mem", [256, 1024])
    hneg_d = dI("hneg", [128, 1])
    ckw = dI("ckw", [16, 2048, 768]); cvw = dI("cvw", [16, 2048, 768]); scv = dI("scv", [16, 30, 768])
    ckm = dI("ckm", [16, 256, 512]); cvm = dI("cvm", [16, 256, 512])
    w_in = dI("w_in", [1024, 9472]); w_mkv = dI("w_mkv", [1024, 1024])
    cpar = dI("cpar", [34, 768])
    w_pa = dI("w_pa", [768, 1024]); w_pb = dI("w_pb", [768, 1024]); w_pc = dI("w_pc", [512, 1024]); w_o = dI("w_o", [1024, 1024])
    gvec = dI("gvec", [3, 1024])
    pbt = dI("pbt", [6, 128, 6, 256])
    sbg = dI("sbg", [128, 9, 48]); sbm = dI("sbm", [128, 9, 48])
    dmk = dI("dmk", [48, 768]); sel = dI("sel", [48, 4])
    yo = dO("yo", [T, 1024]); ys = dO("ys", [TS, 1024])
    kwo = dO("kwo", [T, 768]); vwo = dO("vwo", [T, 768]); cvo = dO("cvo", [30, 768])
    kmo = dO("kmo", [256, 512]); vmo = dO("vmo", [256, 512])
    kso = dO("kso", [TS, 768]); vso = dO("vso", [TS, 768]); cso = dO("cso", [16, 30, 768])
    Bvso = Buf('vso')

    w_in_v = w_in.rearrange("(k p) c -> p k c", p=128)
    w_mkv_v = w_mkv.rearrange("(k p) c -> p k c", p=128)
    w_pa_v = w_pa.rearrange("(k p) c -> p k c", p=128); w_pb_v = w_pb.rearrange("(k p) c -> p k c", p=128)
    w_pc_v = w_pc.rearrange("(k p) c -> p k c", p=128); w_o_v = w_o.rearrange("(k p) c -> p k c", p=128)

    with ExitStack() as es:
        S = Sched(nc, es)
        def sb(stack, name, shape, dt): return stack.enter_context(nc.sbuf_tensor('t_' + name, list(shape), dt))
        def mk(stack, name, shape, dt, dma=False):
            return Tl(sb(stack, name, shape, dt), name, S.dma_sem('d_' + name) if dma else None)
        def ring(stack, name, shape, dt, n, dma=False):
            return Ring([mk(stack, '%s%d' % (name, i), shape, dt, dma) for i in range(n)])
        pbanks = [Tl(es.enter_context(nc.psum_tensor('pb%d' % i, [128, 512], F32)), 'pb%d' % i) for i in range(8)]
        banks = Ring(pbanks[:6]); banksH = Ring(pbanks[6:]); banks8 = Ring(pbanks)
        dconst = S.dma_sem('d_const'); const_bufs = []
        def cload(q, out_ap, in_ap, tl):
            S.dma(q, out_ap, in_ap, dconst, writes=[tl.b]); const_bufs.append(tl.b)
        def const_done():
            for b in const_bufs: b.w = (dconst['key'], dconst['val'])
            del const_bufs[:]
        dout = {}
        def store(q, out_ap, in_ap, tl, key, writes=()):
            if key not in dout: dout[key] = S.dma_sem('do_' + key)
            return S.dma(q, out_ap, in_ap, dout[key], reads=[tl.b], writes=list(writes))
        def load(q, tl, out_ap, in_ap, reads=()):
            return S.dma(q, out_ap, in_ap, tl.d, reads=list(reads), writes=[tl.b])
        def load_parts(q, tl, parts):
            for i, (o, a) in enumerate(parts):
                S.dma(q, o, a, tl.d, writes=[tl.b] if i == 0 else [])
            if not S.halt: tl.b.w = (tl.d['key'], tl.d['val'])

        ident32 = mk(es, 'ident32', [128, 128], F32); identb = mk(es, 'identb', [128, 128], BF16)
        onesb = mk(es, 'onesb', [128, 128], BF16)
        hneg = mk(es, 'hneg', [128, 1], F32)
        xT = mk(es, 'xT', [128, 8, NT], BF16)
        gaT = mk(es, 'gaT', [128, 6, NT], BF16)
        kTs = mk(es, 'kTs', [128, 6, TS], BF16); qTs = mk(es, 'qTs', [128, 6, TS], BF16); zaSs = mk(es, 'zaSs', [128, 6, TS], F32)
        xh30 = mk(es, 'xh30', [128, 8, 32], BF16)
        kmT = mk(es, 'kmT', [128, 4, 256], BF16); vmp = mk(es, 'vmp', [128, 2, 512], BF16)
        ss_r = ring(es, 'ss', [128, 4], F32, 6); rs_r = ring(es, 'rs', [128, 1], F32, 6)

        S.op('pool', lambda e: e.memset(ident32.t[:], 0.0), writes=[ident32.b])
        S.op('pool', lambda e: e.affine_select(out=ident32.t[:], in_=ident32.t[:], pattern=[[-1, 128]], compare_op=ALU.not_equal,
                                               fill=1.0, base=0, channel_multiplier=1), reads=[ident32.b], writes=[ident32.b])
        S.op('pool', lambda e: e.tensor_copy(out=identb.t[:], in_=ident32.t[:]), reads=[ident32.b], writes=[identb.b])
        S.op('pool', lambda e: e.memset(onesb.t[:], 1.0), writes=[onesb.b])
        cload('sp', hneg.t[:], hneg_d[:], hneg)

        def evac(eng, out_ap, in_ap, func=None, scale=None, bias=None):
            if eng == 'act':
                kw = {}
                if scale is not None: kw['scale'] = scale
                if bias is not None: kw['bias'] = bias
                f = func if func is not None else (AF.Identity if bias is not None else AF.Copy)
                return lambda e: e.activation(out=out_ap, in_=in_ap, func=f, **kw)
            assert func is None and bias is None
            if scale is not None:
                return lambda e: e.tensor_scalar(out=out_ap, in0=in_ap, scalar1=float(scale), scalar2=None, op0=ALU.mult)
            return lambda e: e.tensor_copy(out=out_ap, in_=in_ap)

        def ev_op(out_ap, in_ap, rd, wr, func=None, scale=None, bias=None, eng=None):
            if eng is None:
                eng = S.ev() if func is None and bias is None else 'act'
            S.op(eng, evac(eng, out_ap, in_ap, func, scale, bias), reads=rd, writes=wr)

        def mm_fm(bank, M, n, wt, wcol, xt, c0, KC=8):
            def f(e):
                for k in range(KC):
                    i = e.matmul(bank.t[0:M, 0:n], lhsT=wt.t[:, k, wcol:wcol + M], rhs=xt.t[:, k, c0:c0 + n], start=(k == 0), stop=(k == KC - 1))
                return i
            return f

        def mm_tm(bank, rows, ncol, xt, c0, wt, wcol, KC=8):
            def f(e):
                for k in range(KC):
                    i = e.matmul(bank.t[0:rows, 0:ncol], lhsT=xt.t[:, k, c0:c0 + rows], rhs=wt.t[:, k, wcol:wcol + ncol], start=(k == 0), stop=(k == KC - 1))
                return i
            return f

        def dve(fn, rd, wr): S.op('dve', fn, reads=rd, writes=wr)
        def act(fn, rd, wr): S.op('act', fn, reads=rd, writes=wr)
        def pe(fn, rd, wr): S.op('pe', fn, reads=rd, writes=wr)
        def pool(fn, rd, wr): S.op('pool', fn, reads=rd, writes=wr)

        def rstd_from_ss(rs, ssum_ap, rows, n):
            dve(lambda e: e.tensor_scalar(out=rs.t[:rows], in0=ssum_ap, scalar1=1.0 / n, scalar2=EPS, op0=ALU.mult, op1=ALU.add), [], [rs.b])
            act(lambda e: e.activation(out=rs.t[:rows], in_=rs.t[:rows], func=AF.Sqrt), [rs.b], [rs.b])
            dve(lambda e: e.reciprocal(out=rs.t[:rows], in_=rs.t[:rows]), [rs.b], [rs.b])

        GROUPS = [(g * 512, 512) for g in range(4)] + [(T, TS)]

        try:
            with ExitStack() as sA:
                xTh = mk(sA, 'xTh', [128, 8, T], BF16)
                memT = mk(sA, 'memT', [128, 8, 256], BF16)
                with ExitStack() as p1:
                    xin_r = ring(p1, 'xin', [128, 1024], F32, 9, dma=True); xn_r = ring(p1, 'xn', [128, 1024], BF16, 3); rs4_r = ring(p1, 'rs4', [128, 4], F32, 3)
                    junk_r = ring(p1, 'junk', [128, 1024], F32, 2)
                    gb2 = mk(p1, 'gb2', [128, 2, 1024], F32)
                    for i in range(2):
                        cload('sp', gb2.t[:, i, :], gvec[i].partition_broadcast(128), gb2)
                    const_done()
                    def norm_A(grp):
                        ssq = ss_r.next(); sts = []
                        for j, (src, rows, gi, dst, col0) in enumerate(grp):
                            s = xin_r.next(); junk = junk_r.next()
                            load('sp', s, s.t[:rows, :], src)
                            act(lambda e, s=s, junk=junk, rows=rows: e.activation(out=junk.t[:rows, :], in_=s.t[:rows, :], func=AF.Square), [s.b], [junk.b])
                            dve(lambda e, ssq=ssq, junk=junk, rows=rows, j=j: e.tensor_reduce(out=ssq.t[:rows, j:j + 1], in_=junk.t[:rows, :], axis=AX.X, op=ALU.add), [junk.b], [ssq.b])
                            sts.append((s, rows, gi, dst, col0))
                        return (ssq, sts)
                    def norm_B(st):
                        ssq, sts = st
                        rows = sts[0][1]; n = len(sts); rs = rs4_r.next()
                        dve(lambda e: e.tensor_scalar(out=rs.t[:rows, 0:n], in0=ssq.t[:rows, 0:n], scalar1=1.0 / 1024, scalar2=EPS, op0=ALU.mult, op1=ALU.add),
                            [ssq.b], [rs.b])
                        act(lambda e: e.activation(out=rs.t[:rows, 0:n], in_=rs.t[:rows, 0:n], func=AF.Sqrt), [rs.b], [rs.b])
                        dve(lambda e: e.reciprocal(out=rs.t[:rows, 0:n], in_=rs.t[:rows, 0:n]), [rs.b], [rs.b])
                        for j, (s, rows, gi, dst, col0) in enumerate(sts):
                            xn = xn_r.next()
                            dve(lambda e, xn=xn, s=s, rows=rows, j=j, gi=gi: e.scalar_tensor_tensor(out=xn.t[:rows, :], in0=s.t[:rows, :], scalar=rs.t[:rows, j:j + 1], in1=gb2.t[:rows, gi, :],
                                                                 op0=ALU.mult, op1=ALU.mult), [s.b, rs.b, gb2.b], [xn.b])
                            bank = banks.next()
                            pbf = bank.t[:].bitcast(BF16)
                            def tr(e, pbf=pbf, xn=xn, rows=rows):
                                for k in range(8):
                                    i = e.transpose(out=pbf[:, k * 128:k * 128 + rows], in_=xn.t[:rows, k * 128:(k + 1) * 128], identity=identb.t[:rows, :rows])
                                return i
                            pe(tr, [xn.b, identb.b], [bank.b])
                            ev_op(dst.t[:, :, col0:col0 + rows], pbf.rearrange("p (k t) -> p k t", k=8)[:, :, 0:rows], [bank.b], [dst.b])
                    blocks = [(xh[blk * 128:(blk + 1) * 128, :], 128, 0, xTh, blk * 128) for blk in range(16)]
                    blocks += [(xo[blk * 128:(blk + 1) * 128, :], 128, 0, xT, blk * 128) for blk in range(16)]
                    groups_n = [blocks[i:i + 4] for i in range(0, 32, 4)]
                    groups_n.append([(xs[:, :], 64, 0, xT, T)])
                    groups_n.append([(mem[blk * 128:(blk + 1) * 128, :], 128, 1, memT, blk * 128) for blk in range(2)])
                    npend = []
                    for grp in groups_n:
                        npend.append(norm_A(grp))
                        if len(npend) > 1: norm_B(npend.pop(0))
                    while npend: norm_B(npend.pop(0))
                    dve(lambda e: e.tensor_copy(out=xh30.t[:, :, 0:30], in_=xTh.t[:, :, T - 30:T]), [xTh.b], [xh30.b])
                S.barrier()
                if stop == 'norm': S.halt = True
                with ExitStack() as p2:
                    wm = mk(p2, 'wm', [128, 8, 1024], BF16, dma=True)
                    st_r = ring(p2, 'stm', [128, 512], F32, 2)
                    load_parts('pool', wm, [(wm.t[:, :, hh * 512:(hh + 1) * 512], w_mkv_v[:, :, hh * 512:(hh + 1) * 512]) for hh in range(2)])
                    for h in range(4):
                        bank = banks.next()
                        pe(mm_fm(bank, 128, 256, wm, h * 128, memT, 0), [wm.b, memT.b], [bank.b])
                        ev_op(kmT.t[:, h, :], bank.t[:, 0:256], [bank.b], [kmT.b])
                    if stop == 'mem1': S.halt = True
                    for kv in range(2):
                        for blk in range(2):
                            bank = banks.next(); st = st_r.next()
                            pe(mm_tm(bank, 128, 512, memT, blk * 128, wm, kv * 512), [wm.b, memT.b], [bank.b])
                            ev_op(st.t[:, :], bank.t[:, :], [bank.b], [st.b], eng='act')
                            if kv == 1 and VARIANT != 'A':
                                ev_op(vmp.t[:, blk, :], st.t[:, :], [st.b], [vmp.b], eng='dve')
                            dst = (kmo if kv == 0 else vmo)[blk * 128:(blk + 1) * 128, :]
                            store(SQ, dst, st.t[:, :], st, 'stm%d' % ((st_r.i - 1) % 2))
                S.barrier()
                if stop == 'mem': S.halt = True
                with ExitStack() as p25:
                    wr = ring(p25, 'wr', [128, 8, 768], BF16, 2, dma=True)
                    st32_r = ring(p25, 'st32', [128, 768], F32, 3)
                    for kv in range(2):
                        wt = wr.next()
                        load('pool', wt, wt.t[:], w_in_v[:, :, 768 * (kv + 1):768 * (kv + 2)])
                        for blk in range(17):
                            rows = 128 if blk < 16 else 64
                            st = st32_r.next()
                            for half in range(2):
                                bank = banks.next()
                                pe(mm_tm(bank, rows, 384, xT, blk * 128, wt, half * 384), [wt.b, xT.b], [bank.b])
                                ev_op(st.t[:rows, half * 384:(half + 1) * 384], bank.t[:rows, 0:384], [bank.b], [st.b])
                            if blk < 16:
                                dst = (kwo if kv == 0 else vwo)[blk * 128:(blk + 1) * 128, :]; wrt = ()
                            else:
                                dst = (kso if kv == 0 else vso)[:, :]; wrt = (Bvso,) if kv == 1 else ()
                            store('sp', dst, st.t[:rows, :], st, 'st32_%d' % ((st32_r.i - 1) % 3), writes=wrt)
                S.barrier()
                if stop == 'kvtm': S.halt = True
                with ExitStack() as pA:
                    wq_r = ring(pA, 'wq', [128, 8, 128], BF16, 2, dma=True); wk_r = ring(pA, 'wk', [128, 8, 128], BF16, 2, dma=True)
                    wv_r = ring(pA, 'wv', [128, 8, 128], BF16, 2, dma=True); wz_r = ring(pA, 'wz', [128, 8, 128], BF16, 2, dma=True)
                    pb_r = ring(pA, 'pbt', [128, 6, 256], BF16, 2, dma=True)
                    KVQ = [dict(kT=mk(pA, 'kT%d' % i, [128, 2 * T], BF16), vT=mk(pA, 'vT%d' % i, [128, 2 * T], BF16), qT=mk(pA, 'qT%d' % i, [128, 2, T], BF16)) for i in range(2)]
                    acc = mk(pA, 'acc', [128, 2, T], F32)
                    pt_r = ring(pA, 'pt', [128, 256], BF16, 8)
                    vt_r = ring(pA, 'vt', [128, 256], BF16, 8)
                    R_r = ring(pA, 'Rn', [128, 512], F32, 2); zs_r = ring(pA, 'zs', [128, 512], F32, 2)
                    for v in vt_r.tiles:
                        pool(lambda e, v=v: e.memset(v.t[:, 64:192], 1.0), [], [v.b])
                    hmask = mk(pA, 'hmask', [128, 1], F32)
                    vth_r = ring(pA, 'vth', [128, 256], BF16, 4)
                    dve(lambda e: e.tensor_scalar(out=hmask.t[:, :], in0=hneg.t[:, :], scalar1=0.0, scalar2=None, op0=ALU.is_equal), [hneg.b], [hmask.b])
                    for v in vth_r.tiles:
                        dve(lambda e, v=v: e.tensor_scalar(out=v.t[:, 64:192], in0=onesb.t[:, :], scalar1=hmask.t[:, 0:1], scalar2=None, op0=ALU.mult), [onesb.b, hmask.b], [v.b])
                    for i in range(2):
                        pool(lambda e, q=KVQ[i]['qT']: e.memset(q.t[:, :, :], 0.0), [], [KVQ[i]['qT'].b])
                    def load_w(c):
                        W = dict(wq=wq_r.next(), wk=wk_r.next(), wv=wv_r.next(), wz=wz_r.next(), pb=pb_r.next())
                        load('pool', W['wk'], W['wk'].t[:], w_in_v[:, :, 768 + c * 128:768 + (c + 1) * 128])
                        load('pool', W['wv'], W['wv'].t[:], w_in_v[:, :, 1536 + c * 128:1536 + (c + 1) * 128])
                        load('pool', W['wq'], W['wq'].t[:], w_in_v[:, :, c * 128:(c + 1) * 128])
                        load('pool', W['pb'], W['pb'].t[:], pbt[c])
                        load('pool', W['wz'], W['wz'].t[:], w_in_v[:, :, 2304 + c * 128:2304 + (c + 1) * 128])
                        return W
                    def proj_steps(c, W, bufs):
                        for (wt, dstT) in ((W['wk'], bufs['kT']), (W['wv'], bufs['vT'])):
                            for g in range(8):
                                xt, c0 = (xTh, g * 512) if g < 4 else (xT, (g - 4) * 512)
                                bank = banks8.next()
                                pe(mm_fm(bank, 128, 512, wt, 0, xt, c0), [wt.b, xt.b], [bank.b])
                                ev_op(dstT.t[:, g * 512:(g + 1) * 512], bank.t[:, :], [bank.b], [dstT.b])
                                yield
                        wk = W['wk']; wq = W['wq']; qT = bufs['qT']
                        bank = banks8.next()
                        pe(mm_fm(bank, 128, TS, wk, 0, xT, T), [wk.b, xT.b], [bank.b])
                        ev_op(kTs.t[:, c, :], bank.t[:, 0:TS], [bank.b], [kTs.b])
                        for g in range(4):
                            bank = banks8.next()
                            pe(mm_fm(bank, 128, 512, wq, 0, xT, g * 512), [wq.b, xT.b], [bank.b])
                            ev_op(qT.t[0:64, 0, g * 512:(g + 1) * 512], bank.t[0:64, :], [bank.b], [qT.b], scale=0.125, eng='act')
                            ev_op(qT.t[64:128, 1, g * 512:(g + 1) * 512], bank.t[64:128, :], [bank.b], [qT.b], scale=0.125, eng='act')
                            yield
                        bank = banks8.next()
                        pe(mm_fm(bank, 128, TS, wq, 0, xT, T), [wq.b, xT.b], [bank.b])
                        ev_op(qTs.t[:, c, :], bank.t[:, 0:TS], [bank.b], [qTs.b], scale=0.125)
                        yield
                    Ws = [None] * 6
                    Ws[0] = load_w(0)
                    for _ in proj_steps(0, Ws[0], KVQ[0]): pass
                    for c in range(6):
                        W = Ws[c]; bufs = KVQ[c % 2]
                        kT = bufs['kT']; vT = bufs['vT']; qT = bufs['qT']; wz = W['wz']; pb = W['pb']
                        nxt = None
                        if c < 5:
                            Ws[c + 1] = load_w(c + 1)
                            nxt = proj_steps(c + 1, Ws[c + 1], KVQ[(c + 1) % 2])
                        vcache = {}
                        def get_vt(d, r, Bp, vT=vT, vcache=vcache):
                            key = (d, r, Bp)
                            rg = vth_r if Bp < 16 // d else vt_r
                            if key in vcache and vcache[key][1] > rg.i - len(rg.tiles): return vcache[key][0]
                            vt = rg.next(); vcache[key] = (vt, rg.i, rg)
                            s0 = r + d * 128 * Bp
                            bank = banks8.next(); pbf = bank.t[:].bitcast(BF16)
                            pe(lambda e, pbf=pbf, vT=vT, s0=s0, d=d: e.transpose(out=pbf[:, 0:128], in_=vT.t[:, s0:s0 + d * 127 + 1:d], identity=identb.t[:]), [vT.b, identb.b], [bank.b])
                            ev_op(vt.t[:].rearrange("p (a e) -> p a e", e=64)[:, 0:4:3, :], pbf[:, 0:128].rearrange("p (a e) -> p a e", e=64), [bank.b], [vt.b])
                            return vt
                        SK = 4
                        pending = []
                        def S_part(u, kT=kT, qT=qT, pb=pb):
                            bS = banks8.next(); pt = pt_r.next(); u['pt'] = pt
                            def fS(e, bS=bS, k0=u['k0'], qsl=u['qsl'], hl=u['hl'], gi=u['gi'], d=u['d'], pb=pb, kT=kT, qT=qT):
                                for kb in range(2):
                                    e.matmul(bS.t[:, kb * 128:(kb + 1) * 128], lhsT=identb.t[:], rhs=pb.t[:, gi * 2 + hl, kb * 128:(kb + 1) * 128], start=True, stop=False)
                                    i = e.matmul(bS.t[:, kb * 128:(kb + 1) * 128], lhsT=kT.t[:, k0[kb]:k0[kb] + d * 127 + 1:d], rhs=qT.t[:, hl, qsl], start=False, stop=True)
                                return i
                            pe(fS, [identb.b, pb.b, kT.b, qT.b], [bS.b])
                            act(lambda e, bS=bS, pt=pt: e.activation(out=pt.t[:, :], in_=bS.t[:, 0:256], func=AF.Exp), [bS.b], [pt.b])
                        def O_part(u, vcache=vcache):
                            bO = banks8.next(); pt = u['pt']; vts = u['vts']; hl = u['hl']; qsl = u['qsl']
                            for kb in range(2):
                                ent = vcache[u['vkeys'][kb]]; assert ent[0] is vts[kb] and ent[1] > ent[2].i - len(ent[2].tiles)
                            def fO(e, bO=bO, pt=pt, vts=vts, hl=hl):
                                for kb in range(2):
                                    i = e.matmul(bO.t[:, 0:128], lhsT=vts[kb].t[:, hl * 128:(hl + 1) * 128], rhs=pt.t[:, kb * 128:(kb + 1) * 128], start=(kb == 0), stop=(kb == 1))
                                return i
                            pe(fO, [pt.b, vts[0].b, vts[1].b], [bO.b])
                            if u['first']:
                                dve(lambda e, bO=bO, hl=hl, qsl=qsl: e.tensor_copy(out=acc.t[:, hl, qsl], in_=bO.t[:, 0:128]), [bO.b], [acc.b])
                            else:
                                dve(lambda e, bO=bO, hl=hl, qsl=qsl: e.tensor_tensor(out=acc.t[:, hl, qsl], in0=acc.t[:, hl, qsl], in1=bO.t[:, 0:128], op=ALU.add), [bO.b, acc.b], [acc.b])
                        first = True; nu = 0
                        for gi, d in ((2, 16), (1, 4), (0, 1)):
                            B0 = 16 // d
                            for r in range(d):
                                for B in range(B0, 2 * B0):
                                    k0 = [r + d * 128 * (B - 1), r + d * 128 * B]
                                    q0 = r + d * 128 * B - T
                                    qsl = slice(q0, q0 + d * 127 + 1, d)
                                    vts = [get_vt(d, r, B - 1), get_vt(d, r, B)]
                                    for hl in range(2):
                                        u = dict(gi=gi, d=d, k0=k0, qsl=qsl, vts=vts, vkeys=[(d, r, B - 1), (d, r, B)], hl=hl, halo=(B == B0), first=first)
                                        S_part(u); pending.append(u)
                                        if len(pending) > SK: O_part(pending.pop(0))
                                        nu += 1
                                        if nxt is not None and nu % 4 == 0: next(nxt, None)
                            first = False
                        while pending: O_part(pending.pop(0))
                        if nxt is not None:
                            for _ in nxt: pass
                        for g in range(4):
                            sl = slice(g * 512, (g + 1) * 512)
                            bz = banks8.next(); zs = zs_r.next(); R = R_r.next()
                            pe(mm_fm(bz, 128, 512, wz, 0, xT, g * 512), [wz.b, xT.b], [bz.b])
                            act(lambda e, bz=bz, zs=zs: e.activation(out=zs.t[:, :], in_=bz.t[:, :], func=AF.Silu), [bz.b], [zs.b])
                            dve(lambda e, R=R, sl=sl: e.tensor_copy(out=R.t[0:64, :], in_=acc.t[64:128, 0, sl]), [acc.b], [R.b])
                            dve(lambda e, R=R, sl=sl: e.tensor_copy(out=R.t[64:128, :], in_=acc.t[0:64, 1, sl]), [acc.b], [R.b])
                            dve(lambda e, R=R: e.reciprocal(out=R.t[:, :], in_=R.t[:, :]), [R.b], [R.b])
                            dve(lambda e, R=R, zs=zs: e.tensor_tensor(out=R.t[:, :], in0=R.t[:, :], in1=zs.t[:, :], op=ALU.mult), [R.b, zs.b], [R.b])
                            dve(lambda e, R=R, sl=sl, c=c: e.tensor_tensor(out=gaT.t[0:64, c, sl], in0=acc.t[0:64, 0, sl], in1=R.t[0:64, :], op=ALU.mult), [R.b, acc.b], [gaT.b])
                            dve(lambda e, R=R, sl=sl, c=c: e.tensor_tensor(out=gaT.t[64:128, c, sl], in0=acc.t[64:128, 1, sl], in1=R.t[64:128, :], op=ALU.mult), [R.b, acc.b], [gaT.b])
                        bz = banks8.next()
                        pe(mm_fm(bz, 128, TS, wz, 0, xT, T), [wz.b, xT.b], [bz.b])
                        act(lambda e, bz=bz, c=c: e.activation(out=zaSs.t[:, c, :], in_=bz.t[:, 0:TS], func=AF.Silu), [bz.b], [zaSs.b])
                        if stop == 'attn1': S.halt = True
            S.barrier()
            if stop == 'attn': S.halt = True
            with ExitStack() as pB:
                sbt32 = mk(pB, 'sbt32', [128, 9, 48], F32); sbm32 = mk(pB, 'sbm32', [128, 9, 48], F32); sbt = mk(pB, 'sbt', [128, 9, 48], BF16)
                dmk_t = mk(pB, 'dmk', [48, 768], F32); sel_t = mk(pB, 'sel', [48, 4], F32)
                Qbd = mk(pB, 'Qbd', [128, 6, 16, 48], BF16)
                vnew = mk(pB, 'vnew', [4, 16, 769], BF16, dma=True)
                kt_r = ring(pB, 'kt', [128, 768], BF16, 4, dma=True); vt_r = ring(pB, 'svt', [128, 769], BF16, 12, dma=True)
                ktT_r = ring(pB, 'ktT', [128, 6, 128], BF16, 3)
                spt_r = ring(pB, 'spt', [128, 432], BF16, 2)
                Om_r = ring(pB, 'Om', [48, 768], F32, 2); rl_r = ring(pB, 'rl', [48, 1], F32, 2)
                cload('sp', sbt32.t[:], sbg[:], sbt32); cload('sp', sbm32.t[:], sbm[:], sbm32)
                cload('sp', dmk_t.t[:], dmk[:], dmk_t); cload('sp', sel_t.t[:], sel[:], sel_t)
                const_done()
                dve(lambda e: e.tensor_tensor(out=sbt.t[:], in0=sbt32.t[:], in1=sbm32.t[:], op=ALU.add), [sbt32.b, sbm32.b], [sbt.b])
                pool(lambda e: e.memset(Qbd.t[:], 0.0), [], [Qbd.b])
                for c in range(6):
                    for hl in range(2):
                        h = 2 * c + hl
                        dve(lambda e, c=c, hl=hl, h=h: e.tensor_copy(out=Qbd.t[hl * 64:(hl + 1) * 64, c, :, h * 4:(h + 1) * 4],
                                                                      in_=qTs.t[hl * 64:(hl + 1) * 64, c, :].rearrange("p (b t) -> p b t", t=4)), [qTs.b], [Qbd.b])
                for v in vt_r.tiles:
                    pool(lambda e, v=v: e.memset(v.t[:, 768:769], 1.0), [], [v.b])
                pool(lambda e: e.memset(vnew.t[:, :, 768:769], 1.0), [], [vnew.b])
                S.dma('pool', vnew.t[:, :, 0:768], vso.rearrange("(b t) f -> t b f", t=4), vnew.d, reads=[Bvso], writes=[vnew.b])
                for b in range(16):
                    bS = banksH.next(); spt = spt_r.next()
                    vts = []
                    for ti in range(8):
                        kt = kt_r.next(); vt = vt_r.next(); vts.append(vt)
                        if ti < 4:
                            ksrc = ckw[b, ti:2048:16, :]; vsrc = cvw[b, ti:2048:16, :]
                        else:
                            r0 = 1536 + 128 * (ti - 4)
                            ksrc = ckw[b, r0:r0 + 128, :]; vsrc = cvw[b, r0:r0 + 128, :]
                        load('pool', kt, kt.t[:, :], ksrc)
                        load('pool', vt, vt.t[:, 0:768], vsrc)
                        bT = banks.next(); pbf = bT.t[:].bitcast(BF16); ktT = ktT_r.next()
                        def ftr(e, pbf=pbf, kt=kt):
                            for c in range(6):
                                i = e.transpose(out=pbf[:, c * 128:(c + 1) * 128], in_=kt.t[:, c * 128:(c + 1) * 128], identity=identb.t[:])
                            return i
                        pe(ftr, [kt.b, identb.b], [bT.b])
                        ev_op(ktT.t[:, :, :], pbf[:, 0:768].rearrange("p (c k) -> p c k", c=6), [bT.b], [ktT.b])
                        def fS(e, bS=bS, ktT=ktT, ti=ti, b=b):
                            o = bS.t[:, ti * 48:(ti + 1) * 48]
                            e.matmul(o, lhsT=identb.t[:], rhs=sbt.t[:, ti, :], start=True, stop=False)
                            for c in range(6):
                                i = e.matmul(o, lhsT=ktT.t[:, c, :], rhs=Qbd.t[:, c, b, :], start=False, stop=(c == 5))
                            return i
                        pe(fS, [identb.b, sbt.b, ktT.b, Qbd.b], [bS.b])
                    def fS8(e, bS=bS, b=b):
                        o = bS.t[0:4, 384:432]
                        e.matmul(o, lhsT=identb.t[0:4, 0:4], rhs=sbt.t[0:4, 8, :], start=True, stop=False)
                        for c in range(6):
                            i = e.matmul(o, lhsT=kTs.t[:, c, 4 * b:4 * b + 4], rhs=Qbd.t[:, c, b, :], start=False, stop=(c == 5))
                        return i
                    pe(fS8, [identb.b, sbt.b, kTs.b, Qbd.b], [bS.b])
                    act(lambda e, bS=bS, spt=spt: e.activation(out=spt.t[:, 0:384], in_=bS.t[:, 0:384], func=AF.Exp), [bS.b], [spt.b])
                    act(lambda e, bS=bS, spt=spt: e.activation(out=spt.t[0:4, 384:432], in_=bS.t[0:4, 384:432], func=AF.Exp), [bS.b], [spt.b])
                    bO1 = banks.next(); bO2 = banks.next()
                    def fO(e, bO1=bO1, bO2=bO2, spt=spt, vts=vts, b=b):
                        for ti in range(8):
                            e.matmul(bO1.t[0:48, 0:384], lhsT=spt.t[:, ti * 48:(ti + 1) * 48], rhs=vts[ti].t[:, 0:384], start=(ti == 0), stop=False)
                            e.matmul(bO2.t[0:48, 0:385], lhsT=spt.t[:, ti * 48:(ti + 1) * 48], rhs=vts[ti].t[:, 384:769], start=(ti == 0), stop=False)
                        e.matmul(bO1.t[0:48, 0:384], lhsT=spt.t[0:4, 384:432], rhs=vnew.t[0:4, b, 0:384], start=False, stop=True)
                        return e.matmul(bO2.t[0:48, 0:385], lhsT=spt.t[0:4, 384:432], rhs=vnew.t[0:4, b, 384:769], start=False, stop=True)
                    pe(fO, [spt.b, vnew.b] + [v.b for v in vts], [bO1.b, bO2.b])
                    rl = rl_r.next(); Om = Om_r.next()
                    dve(lambda e, rl=rl, bO2=bO2: e.reciprocal(out=rl.t[:, :], in_=bO2.t[0:48, 384:385]), [bO2.b], [rl.b])
                    dve(lambda e, rl=rl, bO1=bO1, Om=Om: e.scalar_tensor_tensor(out=Om.t[:, 0:384], in0=bO1.t[0:48, 0:384], scalar=rl.t[:, 0:1], in1=dmk_t.t[:, 0:384],
                                                                             op0=ALU.mult, op1=ALU.mult), [rl.b, bO1.b, dmk_t.b], [Om.b])
                    dve(lambda e, rl=rl, bO2=bO2, Om=Om: e.scalar_tensor_tensor(out=Om.t[:, 384:768], in0=bO2.t[0:48, 0:384], scalar=rl.t[:, 0:1], in1=dmk_t.t[:, 384:768],
                                                                             op0=ALU.mult, op1=ALU.mult), [rl.b, bO2.b, dmk_t.b], [Om.b])
                    bF = banks.next()
                    def fF(e, bF=bF, Om=Om):
                        for c in range(6):
                            i = e.matmul(bF.t[:, c * 4:(c + 1) * 4], lhsT=Om.t[:, c * 128:(c + 1) * 128], rhs=sel_t.t[:, :], start=True, stop=True)
                        return i
                    pe(fF, [Om.b, sel_t.b], [bF.b])
                    dve(lambda e, bF=bF, b=b: e.tensor_tensor(out=gaT.t[:, :, T + 4 * b:T + 4 * b + 4], in0=bF.t[:, 0:24].rearrange("p (c t) -> p c t", t=4),
                                                              in1=zaSs.t[:, :, 4 * b:4 * b + 4], op=ALU.mult), [bF.b, zaSs.b], [gaT.b])
            S.barrier()
            if stop == 'sattn': S.halt = True
            gcT = mk(es, 'gcT', [128, 4, NT], BF16)
            with ExitStack() as pC:
                wqc = mk(pC, 'wqc', [128, 8, 512], BF16, dma=True); wzc = mk(pC, 'wzc', [128, 8, 512], BF16, dma=True)
                load('pool', wqc, wqc.t[:], w_in_v[:, :, 5376:5888]); load('pool', wzc, wzc.t[:], w_in_v[:, :, 5888:6400])
                qc_r = ring(pC, 'qc', [128, 512], BF16, 3); cpt_r = ring(pC, 'cpt', [128, 2, 512], BF16, 3)
                R_r = ring(pC, 'cR', [128, 512], F32, 2); zs_r = ring(pC, 'czs', [128, 512], F32, 3)
                qcs = mk(pC, 'qcs', [128, 4, TS], BF16); zcs = mk(pC, 'zcs', [128, 4, TS], F32)
                km_r = ring(pC, 'km', [128, 2, 512], BF16, 2, dma=True); vm_r = ring(pC, 'vm', [128, 2, 512], BF16, 2, dma=True)
                kmTs_r = ring(pC, 'kmTs', [128, 4, 256], BF16, 2); spt_r = ring(pC, 'cspt', [128, 32], BF16, 2); sR_r = ring(pC, 'sR', [128, 16], F32, 2)
                SC = 128 ** -0.5
                cpend = []
                def cS_part(h, g):
                    bq = banks8.next(); qc = qc_r.next()
                    pe(mm_fm(bq, 128, 512, wqc, h * 128, xT, g * 512), [wqc.b, xT.b], [bq.b])
                    ev_op(qc.t[:, :], bq.t[:, :], [bq.b], [qc.b], scale=SC)
                    cpt = cpt_r.next(); bs = [banks8.next(), banks8.next()]
                    for m in range(2):
                        pe(lambda e, bm=bs[m], qc=qc, h=h, m=m: e.matmul(bm.t[:, :], lhsT=kmT.t[:, h, m * 128:(m + 1) * 128], rhs=qc.t[:, :], start=True, stop=True),
                           [kmT.b, qc.b], [bs[m].b])
                        act(lambda e, bm=bs[m], cpt=cpt, m=m: e.activation(out=cpt.t[:, m, :], in_=bm.t[:, :], func=AF.Exp), [bs[m].b], [cpt.b])
                    bz = banks8.next(); zs = zs_r.next()
                    pe(mm_fm(bz, 128, 512, wzc, h * 128, xT, g * 512), [wzc.b, xT.b], [bz.b])
                    act(lambda e, bz=bz, zs=zs: e.activation(out=zs.t[:, :], in_=bz.t[:, :], func=AF.Silu), [bz.b], [zs.b])
                    return dict(h=h, g=g, cpt=cpt, zs=zs)
                def cO_part(u):
                    h = u['h']; g = u['g']; cpt = u['cpt']; zs = u['zs']; sl = slice(g * 512, (g + 1) * 512)
                    bO = banks8.next(); bL = banks8.next(); R = R_r.next()
                    def fO(e, bO=bO, bL=bL, cpt=cpt, h=h):
                        for m in range(2):
                            e.matmul(bO.t[:, :], lhsT=vmp.t[:, m, h * 128:(h + 1) * 128], rhs=cpt.t[:, m, :], start=(m == 0), stop=(m == 1))
                        for m in range(2):
                            i = e.matmul(bL.t[:, :], lhsT=onesb.t[:, :], rhs=cpt.t[:, m, :], start=(m == 0), stop=(m == 1))
                        return i
                    pe(fO, [vmp.b, cpt.b, onesb.b], [bO.b, bL.b])
                    dve(lambda e, R=R, bL=bL: e.reciprocal(out=R.t[:, :], in_=bL.t[:, :]), [bL.b], [R.b])
                    dve(lambda e, R=R, zs=zs: e.tensor_tensor(out=R.t[:, :], in0=R.t[:, :], in1=zs.t[:, :], op=ALU.mult), [R.b, zs.b], [R.b])
                    dve(lambda e, R=R, bO=bO, h=h, sl=sl: e.tensor_tensor(out=gcT.t[:, h, sl], in0=bO.t[:, :], in1=R.t[:, :], op=ALU.mult), [R.b, bO.b], [gcT.b])
                for h in range(4):
                    for g in range(4):
                        cpend.append(cS_part(h, g))
                        if len(cpend) > 1: cO_part(cpend.pop(0))
                while cpend: cO_part(cpend.pop(0))
                for h in range(4):
                    bq = banks.next()
                    pe(mm_fm(bq, 128, TS, wqc, h * 128, xT, T), [wqc.b, xT.b], [bq.b])
                    ev_op(qcs.t[:, h, :], bq.t[:, 0:TS], [bq.b], [qcs.b], scale=SC)
                    bz = banks.next()
                    pe(mm_fm(bz, 128, TS, wzc, h * 128, xT, T), [wzc.b, xT.b], [bz.b])
                    act(lambda e, bz=bz, h=h: e.activation(out=zcs.t[:, h, :], in_=bz.t[:, 0:TS], func=AF.Silu), [bz.b], [zcs.b])
                for b in range(16):
                    km = km_r.next(); vm = vm_r.next(); kmTs = kmTs_r.next(); spt = spt_r.next(); sR = sR_r.next()
                    load('pool', km, km.t[:], ckm[b].rearrange("(m p) f -> p m f", p=128))
                    load('pool', vm, vm.t[:], cvm[b].rearrange("(m p) f -> p m f", p=128))
                    bT = banks.next(); pbf = bT.t[:].bitcast(BF16)
                    def ftr(e, pbf=pbf, km=km):
                        for h in range(4):
                            for m in range(2):
                                i = e.transpose(out=pbf[:, h * 256 + m * 128:h * 256 + (m + 1) * 128], in_=km.t[:, m, h * 128:(h + 1) * 128], identity=identb.t[:])
                        return i
                    pe(ftr, [km.b, identb.b], [bT.b])
                    ev_op(kmTs.t[:, :, :], pbf[:, 0:1024].rearrange("p (h k) -> p h k", h=4), [bT.b], [kmTs.b])
                    bS = banks.next()
                    def fS(e, bS=bS, kmTs=kmTs, b=b):
                        for m in range(2):
                            for h in range(4):
                                i = e.matmul(bS.t[:, m * 16 + h * 4:m * 16 + h * 4 + 4], lhsT=kmTs.t[:, h, m * 128:(m + 1) * 128], rhs=qcs.t[:, h, 4 * b:4 * b + 4], start=True, stop=True)
                        return i
                    pe(fS, [kmTs.b, qcs.b], [bS.b])
                    act(lambda e, bS=bS, spt=spt: e.activation(out=spt.t[:, :], in_=bS.t[:, 0:32], func=AF.Exp), [bS.b], [spt.b])
                    bO = banks.next()
                    def fO(e, bO=bO, spt=spt, vm=vm):
                        for h in range(4):
                            for m in range(2):
                                e.matmul(bO.t[:, h * 4:h * 4 + 4], lhsT=vm.t[:, m, h * 128:(h + 1) * 128], rhs=spt.t[:, m * 16 + h * 4:m * 16 + h * 4 + 4], start=(m == 0), stop=(m == 1))
                        for m in range(2):
                            i = e.matmul(bO.t[:, 16:32], lhsT=onesb.t[:, :], rhs=spt.t[:, m * 16:(m + 1) * 16], start=(m == 0), stop=(m == 1))
                        return i
                    pe(fO, [spt.b, vm.b, onesb.b], [bO.b])
                    dve(lambda e, sR=sR, bO=bO: e.reciprocal(out=sR.t[:, :], in_=bO.t[:, 16:32]), [bO.b], [sR.b])
                    dve(lambda e, sR=sR, b=b: e.tensor_tensor(out=sR.t[:, :].rearrange("p (h t) -> p h t", t=4), in0=sR.t[:, :].rearrange("p (h t) -> p h t", t=4),
                                                              in1=zcs.t[:, :, 4 * b:4 * b + 4], op=ALU.mult), [sR.b, zcs.b], [sR.b])
                    dve(lambda e, sR=sR, bO=bO, b=b: e.tensor_tensor(out=gcT.t[:, :, T + 4 * b:T + 4 * b + 4], in0=bO.t[:, 0:16].rearrange("p (h t) -> p h t", t=4),
                                                                     in1=sR.t[:, :].rearrange("p (h t) -> p h t", t=4), op=ALU.mult), [sR.b, bO.b], [gcT.b])
            S.barrier()
            if stop == 'cattn': S.halt = True
            gbT = mk(es, 'gbT', [128, 6, NT], BF16)
            with ExitStack() as pD:
                convb = mk(pD, 'convb', [128, 6, NT], BF16)
                cwT = mk(pD, 'cwT', [128, 6, 34], F32)
                with ExitStack() as pD1:
                    cw_tm = mk(pD1, 'cw_tm', [34, 768], F32)
                    hs = [mk(pD1, 'hs%d' % i, [120, 768], F32) for i in range(4)]
                    wu1_r = ring(pD1, 'wu1', [128, 8, 128], BF16, 2, dma=True); wu2_r = ring(pD1, 'wu2', [128, 8, 128], BF16, 2, dma=True)
                    G_r = ring(pD1, 'G', [128, 30 + T], BF16, 2); GS_r = ring(pD1, 'GS', [128, 16, 34], F32, 2)
                    GSb_r = ring(pD1, 'GSb', [128, 16, 34], BF16, 2); Gl_r = ring(pD1, 'Gl', [128, 32], F32, 2); Dg_r = ring(pD1, 'Dg', [128, 31, 128], BF16, 2)
                    sg_r = ring(pD1, 'sg', [128, 512], F32, 2); gn_r = ring(pD1, 'gn', [128, 64], F32, 2)
                    cvst = mk(pD1, 'cvst', [30, 768], F32); csst = mk(pD1, 'csst', [64, 768], F32)
                    cload('sp', cw_tm.t[:], cpar[:], cw_tm)
                    for i in range(4):
                        cload('sp', hs[i].t[:], scv[4 * i:4 * i + 4].rearrange("b r f -> (b r) f"), hs[i])
                    const_done()
                    bank = banks.next()
                    def ftr(e, bank=bank):
                        for c in range(6):
                            i = e.transpose(out=bank.t[:, c * 34:(c + 1) * 34], in_=cw_tm.t[0:34, c * 128:(c + 1) * 128], identity=ident32.t[0:34, 0:34])
                        return i
                    pe(ftr, [cw_tm.b, ident32.b], [bank.b])
                    ev_op(cwT.t[:, :, :], bank.t[:, 0:204].rearrange("p (c j) -> p c j", c=6), [bank.b], [cwT.b], eng='dve')
                    for b in range(16):
                        i, bb = b // 4, b % 4
                        store('sp', cso[b, 0:26, :], hs[i].t[bb * 30 + 4:bb * 30 + 30, :], hs[i], 'cso_old')
                    for c in range(6):
                        wu1 = wu1_r.next(); wu2 = wu2_r.next(); G = G_r.next(); GS = GS_r.next(); GSb = GSb_r.next(); Gl = Gl_r.next(); Dg = Dg_r.next()
                        for j in range(31):
                            dve(lambda e, Dg=Dg, c=c, j=j: e.tensor_scalar(out=Dg.t[:, j, :], in0=identb.t[:, :], scalar1=cwT.t[:, c, j:j + 1], scalar2=None, op0=ALU.mult), [identb.b, cwT.b], [Dg.b])
                        load('pool', wu1, wu1.t[:], w_in_v[:, :, 3072 + c * 128:3072 + (c + 1) * 128])
                        load('pool', wu2, wu2.t[:], w_in_v[:, :, 3840 + c * 128:3840 + (c + 1) * 128])
                        bank = banks.next()
                        def fh(e, bank=bank, c=c):
                            for i in range(4):
                                r = e.transpose(out=bank.t[:, i * 120:(i + 1) * 120], in_=hs[i].t[0:120, c * 128:(c + 1) * 128], identity=ident32.t[0:120, 0:120])
                            return r
                        pe(fh, [hs[0].b, hs[1].b, hs[2].b, hs[3].b, ident32.b], [bank.b])
                        ev_op(GS.t[:, :, 0:30], bank.t[:, 0:480].rearrange("p (b r) -> p b r", r=30), [bank.b], [GS.b], eng='dve')
                        glist = [(xh30, 0, 30, G.t[:, 0:30])] + [(xT, g * 512, 512, G.t[:, 30 + g * 512:30 + (g + 1) * 512]) for g in range(4)] + [(xT, T, TS, None)]
                        for (xt, c0, n, dst) in glist:
                            b1 = banks.next(); b2 = banks.next(); sg = sg_r.next()
                            pe(mm_fm(b1, 128, n, wu1, 0, xt, c0), [wu1.b, xt.b], [b1.b])
                            pe(mm_fm(b2, 128, n, wu2, 0, xt, c0), [wu2.b, xt.b], [b2.b])
                            act(lambda e, b2=b2, sg=sg, n=n: e.activation(out=sg.t[:, 0:n], in_=b2.t[:, 0:n], func=AF.Sigmoid), [b2.b], [sg.b])
                            if dst is not None:
                                dve(lambda e, b1=b1, sg=sg, n=n, dst=dst: e.tensor_tensor(out=dst, in0=b1.t[:, 0:n], in1=sg.t[:, 0:n], op=ALU.mult), [b1.b, sg.b], [G.b])
                                if c0 == 3 * 512 and n == 512:
                                    dve(lambda e, b1=b1, sg=sg, Gl=Gl: e.tensor_tensor(out=Gl.t[:, 0:30], in0=b1.t[:, 482:512], in1=sg.t[:, 482:512], op=ALU.mult), [b1.b, sg.b], [Gl.b])
                            else:
                                dve(lambda e, b1=b1, sg=sg, GS=GS: e.tensor_tensor(out=GS.t[:, :, 30:34], in0=b1.t[:, 0:TS].rearrange("p (b t) -> p b t", t=4),
                                                                                     in1=sg.t[:, 0:TS].rearrange("p (b t) -> p b t", t=4), op=ALU.mult), [b1.b, sg.b], [GS.b])
                        dve(lambda e, GS=GS, GSb=GSb: e.tensor_copy(out=GSb.t[:, :, :], in_=GS.t[:, :, :]), [GS.b], [GSb.b])
                        for g in range(4):
                            bank = banks.next()
                            def fc(e, bank=bank, Dg=Dg, G=G, g=g):
                                for j in range(31):
                                    i = e.matmul(bank.t[:, :], lhsT=Dg.t[:, j, :], rhs=G.t[:, g * 512 + j:g * 512 + j + 512], start=(j == 0), stop=(j == 30))
                                return i
                            pe(fc, [Dg.b, G.b], [bank.b])
                            act(lambda e, bank=bank, c=c, g=g: e.activation(out=convb.t[:, c, g * 512:(g + 1) * 512], in_=bank.t[:, :], func=AF.Identity, bias=cwT.t[:, c, 31:32]), [bank.b, cwT.b], [convb.b])
                        bank = banks.next()
                        def fcs(e, bank=bank, Dg=Dg, GSb=GSb):
                            for j in range(31):
                                i = e.matmul(bank.t[:, 0:64].rearrange("p (b t) -> p b t", t=4), lhsT=Dg.t[:, j, :], rhs=GSb.t[:, :, j:j + 4], start=(j == 0), stop=(j == 30))
                            return i
                        pe(fcs, [Dg.b, GSb.b], [bank.b])
                        act(lambda e, bank=bank, c=c: e.activation(out=convb.t[:, c, T:NT], in_=bank.t[:, 0:64], func=AF.Identity, bias=cwT.t[:, c, 31:32]), [bank.b, cwT.b], [convb.b])
                        bank = banks.next(); gn = gn_r.next()
                        pe(lambda e, bank=bank, Gl=Gl: e.transpose(out=bank.t[0:30, 0:128], in_=Gl.t[:, 0:30], identity=ident32.t[:, :]), [Gl.b, ident32.b], [bank.b])
                        ev_op(cvst.t[:, c * 128:(c + 1) * 128], bank.t[0:30, 0:128], [bank.b], [cvst.b], eng='act')
                        dve(lambda e, gn=gn, GS=GS: e.tensor_copy(out=gn.t[:, :].rearrange("p (b t) -> p b t", t=4), in_=GS.t[:, :, 30:34]), [GS.b], [gn.b])
                        bank = banks.next()
                        pe(lambda e, bank=bank, gn=gn: e.transpose(out=bank.t[0:64, 0:128], in_=gn.t[:, :], identity=ident32.t[:, :]), [gn.b, ident32.b], [bank.b])
                        ev_op(csst.t[:, c * 128:(c + 1) * 128], bank.t[0:64, 0:128], [bank.b], [csst.b], eng='act')
                    store('sp', cvo[:, :], cvst.t[:, :], cvst, 'cvo')
                    for b in range(16):
                        store('sp', cso[b, 26:30, :], csst.t[4 * b:4 * b + 4, :], csst, 'cso_new')
                S.barrier()
                wzb = mk(pD, 'wzb', [128, 8, 768], BF16, dma=True)
                load('pool', wzb, wzb.t[:], w_in_v[:, :, 4608:5376])
                mean_r = ring(pD, 'mean', [128, 512], F32, 2); rstd_r = ring(pD, 'rstd', [128, 512], F32, 2); msq_r = ring(pD, 'msq', [128, 512], F32, 2)
                sq_r = ring(pD, 'sq', [128, 512], BF16, 3); t_r = ring(pD, 'lt', [128, 512], F32, 3); zs_r = ring(pD, 'bzs', [128, 512], F32, 2)
                for (c0, n) in GROUPS:
                    bsum = banksH.next(); bsq = banksH.next()
                    def fsum(e, bsum=bsum, c0=c0, n=n):
                        for c in range(6):
                            i = e.matmul(bsum.t[:, 0:n], lhsT=onesb.t[:, :], rhs=convb.t[:, c, c0:c0 + n], start=(c == 0), stop=(c == 5))
                        return i
                    pe(fsum, [onesb.b, convb.b], [bsum.b])
                    for c in range(6):
                        sq = sq_r.next()
                        act(lambda e, sq=sq, c=c, c0=c0, n=n: e.activation(out=sq.t[:, 0:n], in_=convb.t[:, c, c0:c0 + n], func=AF.Square), [convb.b], [sq.b])
                        pe(lambda e, bsq=bsq, sq=sq, c=c, n=n: e.matmul(bsq.t[:, 0:n], lhsT=onesb.t[:, :], rhs=sq.t[:, 0:n], start=(c == 0), stop=(c == 5)), [onesb.b, sq.b], [bsq.b])
                    mean = mean_r.next(); rstd = rstd_r.next(); msq = msq_r.next()
                    dve(lambda e, mean=mean, bsum=bsum, n=n: e.tensor_scalar(out=mean.t[:, 0:n], in0=bsum.t[:, 0:n], scalar1=1.0 / 768, scalar2=None, op0=ALU.mult), [bsum.b], [mean.b])
                    dve(lambda e, mean=mean, msq=msq, n=n: e.tensor_tensor(out=msq.t[:, 0:n], in0=mean.t[:, 0:n], in1=mean.t[:, 0:n], op=ALU.mult), [mean.b], [msq.b])
                    dve(lambda e, rstd=rstd, bsq=bsq, msq=msq, n=n: e.scalar_tensor_tensor(out=rstd.t[:, 0:n], in0=bsq.t[:, 0:n], scalar=1.0 / 768, in1=msq.t[:, 0:n], op0=ALU.mult, op1=ALU.subtract),
                        [bsq.b, msq.b], [rstd.b])
                    dve(lambda e, rstd=rstd, n=n: e.tensor_scalar(out=rstd.t[:, 0:n], in0=rstd.t[:, 0:n], scalar1=EPS, scalar2=None, op0=ALU.add), [rstd.b], [rstd.b])
                    act(lambda e, rstd=rstd, n=n: e.activation(out=rstd.t[:, 0:n], in_=rstd.t[:, 0:n], func=AF.Sqrt), [rstd.b], [rstd.b])
                    dve(lambda e, rstd=rstd, n=n: e.reciprocal(out=rstd.t[:, 0:n], in_=rstd.t[:, 0:n]), [rstd.b], [rstd.b])
                    for c in range(6):
                        bz = banks.next(); zs = zs_r.next(); t = t_r.next()
                        pe(mm_fm(bz, 128, n, wzb, c * 128, xT, c0), [wzb.b, xT.b], [bz.b])
                        act(lambda e, bz=bz, zs=zs, n=n: e.activation(out=zs.t[:, 0:n], in_=bz.t[:, 0:n], func=AF.Silu), [bz.b], [zs.b])
                        dve(lambda e, t=t, c=c, c0=c0, n=n, mean=mean: e.tensor_tensor(out=t.t[:, 0:n], in0=convb.t[:, c, c0:c0 + n], in1=mean.t[:, 0:n], op=ALU.subtract), [convb.b, mean.b], [t.b])
                        dve(lambda e, t=t, n=n, rstd=rstd: e.tensor_tensor(out=t.t[:, 0:n], in0=t.t[:, 0:n], in1=rstd.t[:, 0:n], op=ALU.mult), [t.b, rstd.b], [t.b])
                        dve(lambda e, t=t, n=n, c=c: e.tensor_scalar(out=t.t[:, 0:n], in0=t.t[:, 0:n], scalar1=cwT.t[:, c, 32:33], scalar2=cwT.t[:, c, 33:34], op0=ALU.mult, op1=ALU.add), [t.b, cwT.b], [t.b])
                        act(lambda e, t=t, n=n: e.activation(out=t.t[:, 0:n], in_=t.t[:, 0:n], func=AF.Silu), [t.b], [t.b])
                        dve(lambda e, t=t, n=n, zs=zs, c=c, c0=c0: e.tensor_tensor(out=gbT.t[:, c, c0:c0 + n], in0=t.t[:, 0:n], in1=zs.t[:, 0:n], op=ALU.mult), [t.b, zs.b], [gbT.b])
            S.barrier()
            if stop == 'conv': S.halt = True
            mixT = mk(es, 'mixT', [128, 8, NT], BF16)
            wo = mk(es, 'wo', [128, 8, 1024], BF16, dma=True)
            load_parts('pool', wo, [(wo.t[:, :, hh * 512:(hh + 1) * 512], w_o_v[:, :, hh * 512:(hh + 1) * 512]) for hh in range(2)])
            with ExitStack() as pE:
                wg_r = [ring(pE, 'wg%d' % i, [128, 8, 128], BF16, 2, dma=True) for i in range(3)]
                wpa_r = ring(pE, 'wpa', [128, 6, 128], BF16, 2, dma=True); wpb_r = ring(pE, 'wpb', [128, 6, 128], BF16, 2, dma=True)
                wpc_r = ring(pE, 'wpc', [128, 4, 128], BF16, 2, dma=True)
                sg_r = ring(pE, 'msg', [128, 512], F32, 6); t_r = ring(pE, 'mt', [128, 512], F32, 4)
                for k in range(8):
                    ks = slice(k * 128, (k + 1) * 128)
                    wg = [r.next() for r in wg_r]; wpa = wpa_r.next(); wpb = wpb_r.next(); wpc = wpc_r.next()
                    for i in range(3):
                        load('pool', wg[i], wg[i].t[:], w_in_v[:, :, 6400 + i * 1024 + k * 128:6400 + i * 1024 + (k + 1) * 128])
                    load('pool', wpa, wpa.t[:], w_pa_v[:, :, ks]); load('pool', wpb, wpb.t[:], w_pb_v[:, :, ks]); load('pool', wpc, wpc.t[:], w_pc_v[:, :, ks])
                    for (c0, n) in GROUPS:
                        bP = [banks.next() for _ in range(3)]; bG = [banks.next() for _ in range(3)]
                        pe(mm_fm(bP[0], 128, n, wpa, 0, gaT, c0, KC=6), [wpa.b, gaT.b], [bP[0].b])
                        pe(mm_fm(bP[1], 128, n, wpb, 0, gbT, c0, KC=6), [wpb.b, gbT.b], [bP[1].b])
                        pe(mm_fm(bP[2], 128, n, wpc, 0, gcT, c0, KC=4), [wpc.b, gcT.b], [bP[2].b])
                        sgs = []
                        for i in range(3):
                            pe(mm_fm(bG[i], 128, n, wg[i], 0, xT, c0), [wg[i].b, xT.b], [bG[i].b])
                            sg = sg_r.next(); sgs.append(sg)
                            act(lambda e, bg=bG[i], sg=sg, n=n: e.activation(out=sg.t[:, 0:n], in_=bg.t[:, 0:n], func=AF.Sigmoid), [bG[i].b], [sg.b])
                        t = t_r.next(); t2 = t_r.next()
                        dve(lambda e, t=t, sg=sgs[0], bp=bP[0], n=n: e.tensor_tensor(out=t.t[:, 0:n], in0=sg.t[:, 0:n], in1=bp.t[:, 0:n], op=ALU.mult), [sgs[0].b, bP[0].b], [t.b])
                        dve(lambda e, t2=t2, sg=sgs[1], bp=bP[1], n=n: e.tensor_tensor(out=t2.t[:, 0:n], in0=sg.t[:, 0:n], in1=bp.t[:, 0:n], op=ALU.mult), [sgs[1].b, bP[1].b], [t2.b])
                        dve(lambda e, t=t, t2=t2, n=n: e.tensor_tensor(out=t.t[:, 0:n], in0=t.t[:, 0:n], in1=t2.t[:, 0:n], op=ALU.add), [t.b, t2.b], [t.b])
                        dve(lambda e, t2=t2, sg=sgs[2], bp=bP[2], n=n: e.tensor_tensor(out=t2.t[:, 0:n], in0=sg.t[:, 0:n], in1=bp.t[:, 0:n], op=ALU.mult), [sgs[2].b, bP[2].b], [t2.b])
                        dve(lambda e, t=t, t2=t2, n=n, k=k, c0=c0: e.tensor_tensor(out=mixT.t[:, k, c0:c0 + n], in0=t.t[:, 0:n], in1=t2.t[:, 0:n], op=ALU.add), [t.b, t2.b], [mixT.b])
            S.barrier()
            if stop == 'merge': S.halt = True
            with ExitStack() as pF:
                gpost = mk(pF, 'gpost', [128, 1024], F32)
                xin_r = ring(pF, 'xin2', [128, 1024], F32, 4, dma=True); y_r = ring(pF, 'yout', [128, 1024], F32, 2)
                junk_r = ring(pF, 'junk2', [128, 1024], F32, 2)
                cload('sp', gpost.t[:], gvec[2].partition_broadcast(128), gpost)
                const_done()
                def fin_A(blk):
                    rows = 128 if blk < 16 else 64
                    xi = xin_r.next(); ssq = ss_r.next(); junk = junk_r.next()
                    load('sp', xi, xi.t[:rows, :], xo[blk * 128:(blk + 1) * 128, :] if blk < 16 else xs[:, :])
                    bb = [banks.next(), banks.next()]
                    for half in range(2):
                        pe(mm_tm(bb[half], rows, 512, mixT, blk * 128, wo, half * 512), [mixT.b, wo.b], [bb[half].b])
                        act(lambda e, bk=bb[half], half=half, rows=rows, junk=junk: e.activation(out=junk.t[:rows, half * 512:(half + 1) * 512], in_=bk.t[:rows, :], func=AF.Square), [bb[half].b], [junk.b])
                    dve(lambda e, ssq=ssq, rows=rows, junk=junk: e.tensor_reduce(out=ssq.t[:rows, 0:1], in_=junk.t[:rows, :], axis=AX.X, op=ALU.add), [junk.b], [ssq.b])
                    return (blk, rows, xi, ssq, bb)
                def fin_B(st):
                    blk, rows, xi, ssq, bb = st
                    y = y_r.next(); rs = rs_r.next()
                    dve(lambda e, ssq=ssq, rs=rs, rows=rows: e.tensor_scalar(out=rs.t[:rows], in0=ssq.t[:rows, 0:1], scalar1=1.0 / 1024, scalar2=EPS, op0=ALU.mult, op1=ALU.add), [ssq.b], [rs.b])
                    act(lambda e, rs=rs, rows=rows: e.activation(out=rs.t[:rows], in_=rs.t[:rows], func=AF.Sqrt), [rs.b], [rs.b])
                    dve(lambda e, rs=rs, rows=rows: e.reciprocal(out=rs.t[:rows], in_=rs.t[:rows]), [rs.b], [rs.b])
                    for half in range(2):
                        hs_ = slice(half * 512, (half + 1) * 512)
                        dve(lambda e, y=y, bk=bb[half], rs=rs, rows=rows, hs_=hs_: e.scalar_tensor_tensor(out=y.t[:rows, hs_], in0=bk.t[:rows, :], scalar=rs.t[:rows, 0:1], in1=gpost.t[:rows, hs_],
                                                                                                       op0=ALU.mult, op1=ALU.mult), [bb[half].b, rs.b, gpost.b], [y.b])
                    dve(lambda e, y=y, xi=xi, rows=rows: e.tensor_tensor(out=y.t[:rows, :], in0=y.t[:rows, :], in1=xi.t[:rows, :], op=ALU.add), [y.b, xi.b], [y.b])
                    store('sp', yo[blk * 128:(blk + 1) * 128, :] if blk < 16 else ys[:, :], y.t[:rows, :], y, 'y%d' % ((y_r.i - 1) % 2))
                fpend = []
                for blk in range(17):
                    fpend.append(fin_A(blk))
                    if len(fpend) > 1: fin_B(fpend.pop(0))
                while fpend: fin_B(fpend.pop(0))
        except StopBuild:
            pass
        S.emit()
    return nc


def host_inputs(inp):
    f = lambda a: np.ascontiguousarray(np.asarray(a, dtype=np.float32))
    rb = f(inp['rel_bias'])
    pidx, pvalid = prompt_bias_tables()
    pbt = np.zeros((6, 128, 6, 256), np.float32)
    for c in range(6):
        for gi in range(3):
            for hl in range(2):
                pbt[c, :, gi * 2 + hl, :] = np.where(pvalid, rb[pidx[gi], 2 * c + hl], np.float32(NEG))
    sidx, sadd = sample_bias_tables()
    sbg = np.zeros((128, 9, 48), np.float32); sbm = np.zeros((128, 9, 48), np.float32)
    for ti in range(9):
        for h in range(12):
            sbg[:, ti, h * 4:(h + 1) * 4] = np.where(sadd[ti] > -1e29, rb[sidx[ti], h], np.float32(0.0))
            sbm[:, ti, h * 4:(h + 1) * 4] = sadd[ti]
    dmk = np.zeros((48, 12, 64), np.float32)
    for h in range(12): dmk[h * 4:(h + 1) * 4, h, :] = 1.0
    sel = np.zeros((48, 4), np.float32)
    for h in range(12):
        for t in range(4): sel[h * 4 + t, t] = 1.0
    shared = {
        'w_in': f(inp['w_in'][0]), 'w_mkv': f(inp['w_mem_kv'][0]),
        'cpar': f(np.concatenate([inp['conv_w'][0], inp['conv_b'], inp['ln_g'], inp['ln_b']], axis=0)),
        'w_pa': f(inp['w_proj_a'][0]), 'w_pb': f(inp['w_proj_b'][0]), 'w_pc': f(inp['w_proj_c'][0]), 'w_o': f(inp['w_out'][0]),
        'gvec': f(np.concatenate([inp['g_pre'], inp['g_mem'], inp['g_post']], axis=0)),
        'pbt': pbt, 'sbg': sbg, 'sbm': sbm, 'dmk': dmk.reshape(48, 768), 'sel': sel,
    }
    xp = np.asarray(inp['x_prompt']); xsm = np.asarray(inp['x_sample'])
    maps = []
    for c in range(8):
        b, hf = c // 2, c % 2
        m = dict(shared)
        m['xo'] = f(xp[b, hf * T:(hf + 1) * T])
        m['xh'] = f(xp[b, 0:T]) if hf == 1 else np.zeros((T, 1024), np.float32)
        m['xs'] = f(xsm[16 * c:16 * c + 16].reshape(TS, 1024))
        m['mem'] = f(inp['mem_prompt'][b])
        m['hneg'] = np.full((128, 1), 0.0 if hf == 1 else NEG, np.float32)
        m['ckw'] = f(np.asarray(inp['cache_k_win'])[0, 16 * c:16 * c + 16].reshape(16, 2048, 768))
        m['cvw'] = f(np.asarray(inp['cache_v_win'])[0, 16 * c:16 * c + 16].reshape(16, 2048, 768))
        m['scv'] = f(np.asarray(inp['state_conv'])[0, 16 * c:16 * c + 16])
        m['ckm'] = f(np.asarray(inp['cache_k_mem'])[0, 16 * c:16 * c + 16].reshape(16, 256, 512))
        m['cvm'] = f(np.asarray(inp['cache_v_mem'])[0, 16 * c:16 * c + 16].reshape(16, 256, 512))
        maps.append(m)
    return maps


_NC = None


def kernel(**inp):
    global _NC
    if _NC is None:
        _NC = build()
    maps = host_inputs(inp)
    res = run_bass_kernel_spmd(_NC, maps, core_ids=list(range(8)))
    R = res.results
    y_p = np.stack([np.concatenate([R[2 * b]['yo'], R[2 * b + 1]['yo']], axis=0) for b in range(4)])
    y_s = np.concatenate([R[c]['ys'].reshape(16, 4, 1024) for c in range(8)], axis=0)
    kwp = np.stack([R[2 * b + 1]['kwo'].reshape(T, 12, 64) for b in range(4)])[None]
    vwp = np.stack([R[2 * b + 1]['vwo'].reshape(T, 12, 64) for b in range(4)])[None]
    cvp = np.stack([R[2 * b + 1]['cvo'] for b in range(4)])[None]
    kmp = np.stack([R[2 * b]['kmo'].reshape(256, 4, 128) for b in range(4)])[None]
    vmp = np.stack([R[2 * b]['vmo'].reshape(256, 4, 128) for b in range(4)])[None]
    kws = np.concatenate([R[c]['kso'].reshape(16, 4, 12, 64) for c in range(8)], axis=0)[None]
    vws = np.concatenate([R[c]['vso'].reshape(16, 4, 12, 64) for c in range(8)], axis=0)[None]
    cvs = np.concatenate([R[c]['cso'] for c in range(8)], axis=0)[None]
    return tuple(np.ascontiguousarray(a, dtype=np.float32) for a in (y_p, y_s, kwp, vwp, cvp, kmp, vmp, kws, vws, cvs))
```

```python
import numpy as np
import concourse.bass as bass
import concourse.mybir as mybir
from concourse.bass_utils import run_bass_kernel_spmd
from contextlib import ExitStack

F32 = mybir.dt.float32; BF16 = mybir.dt.bfloat16
AF = mybir.ActivationFunctionType; ALU = mybir.AluOpType; AX = mybir.AxisListType
NEG = -1e30; EPS = 1e-6
T = 2048; TS = 64; NT = T + TS
PATS = (1, 4, 16)


class Buf:
    __slots__ = ('name', 'w', 'r')
    def __init__(self, name='b'):
        self.name = name; self.w = None; self.r = []


class Tl:
    def __init__(self, t, name, d=None):
        self.t = t; self.b = Buf(name); self.d = d


class Sched:
    ENG = ['pe', 'act', 'dve', 'pool', 'sp']
    def __init__(self, nc, es):
        self.nc = nc; self.es = es
        self.ops = {e: [] for e in self.ENG}
        self.cnt = {e: 0 for e in self.ENG}
        self.sems = []
        self.esem = {}
        for e in self.ENG:
            self.esem[e] = self.new_sem('s_' + e)
        self.known = {e: {} for e in self.ENG}
        self.dsems = []
        self.nev = 0
        self.halt = False
    def new_sem(self, name):
        s = self.es.enter_context(self.nc.semaphore(name))
        self.sems.append(s)
        return len(self.sems) - 1
    def dma_sem(self, name):
        d = {'key': self.new_sem(name), 'val': 0}
        self.dsems.append(d)
        return d
    def _waits(self, eng, reads, writes):
        deps = {}
        def add(t):
            if t is None: return
            k, v = t
            if deps.get(k, 0) < v: deps[k] = v
        for b in reads: add(b.w)
        for b in writes:
            add(b.w)
            for t in b.r: add(t)
        waits = []
        kn = self.known[eng]
        for k, v in deps.items():
            if eng == 'pe' and k == self.esem['pe']: continue
            if kn.get(k, 0) >= v: continue
            kn[k] = v
            waits.append((k, v))
        return waits
    def op(self, eng, fn, reads=(), writes=()):
        if self.halt: return None
        waits = self._waits(eng, reads, writes)
        self.cnt[eng] += 1
        tok = (self.esem[eng], self.cnt[eng])
        self.ops[eng].append((waits, fn, ('c', self.esem[eng])))
        for b in reads: b.r.append(tok)
        for b in writes: b.w = tok; b.r = []
        return tok
    def dma(self, q, out_ap, in_ap, dsem, reads=(), writes=()):
        if self.halt: return None
        waits = self._waits(q, reads, writes)
        dsem['val'] += 16
        tok = (dsem['key'], dsem['val'])
        self.ops[q].append((waits, lambda e, o=out_ap, i=in_ap: e.dma_start(out=o, in_=i), ('d', dsem['key'])))
        for b in reads: b.r.append(tok)
        for b in writes: b.w = tok; b.r = []
        return tok
    def ev(self):
        self.nev += 1
        return 'act' if self.nev % 2 else 'dve'
    def barrier(self):
        fin = [(self.esem[e], self.cnt[e]) for e in self.ENG if self.cnt[e] > 0]
        fin += [(d['key'], d['val']) for d in self.dsems if d['val'] > 0]
        for e in self.ENG:
            kn = self.known[e]; w = []
            for k, v in fin:
                if kn.get(k, 0) >= v: continue
                kn[k] = v; w.append((k, v))
            self.ops[e].append((w, None, None))
    def emit(self):
        nc = self.nc
        fin = [(self.esem[e], self.cnt[e]) for e in self.ENG if self.cnt[e] > 0]
        fin += [(d['key'], d['val']) for d in self.dsems if d['val'] > 0]
        def run(eng_obj, name):
            for waits, fn, tok in self.ops[name]:
                for k, v in waits:
                    eng_obj.wait_ge(self.sems[k], v)
                if fn is None: continue
                inst = fn(eng_obj)
                if tok[0] == 'c': inst.then_inc(self.sems[tok[1]], 1)
                else: inst.then_inc(self.sems[tok[1]], 16)
            if name == 'sp':
                for k, v in fin: eng_obj.wait_ge(self.sems[k], v)
        with nc.Block() as block:
            @block.sync
            def _(e): run(e, 'sp')
            @block.tensor
            def _(e): run(e, 'pe')
            @block.scalar
            def _(e): run(e, 'act')
            @block.vector
            def _(e): run(e, 'dve')
            @block.gpsimd
            def _(e): run(e, 'pool')


class StopBuild(Exception):
    pass


class Ring:
    def __init__(self, tiles):
        self.tiles = tiles; self.i = 0
    def next(self):
        t = self.tiles[self.i % len(self.tiles)]; self.i += 1
        return t


def t5_buckets(n):
    NB = 32; exact = 16
    nf = np.maximum(n, 1).astype(np.float32)
    scale = np.float32(NB - exact) / np.log(np.float32(2048) / np.float32(exact))
    large = exact + (np.log(nf / np.float32(exact)) * scale).astype(np.int32)
    large = np.minimum(large, NB - 1)
    return np.where(n < exact, n, large).astype(np.int32)


def prompt_bias_tables():
    k = np.arange(128)[:, None]; c = np.arange(256)[None, :]
    q = c % 128
    dist = np.where(c < 128, q - k + 128, q - k)
    valid = (dist >= 0) & (dist <= 128)
    idx = np.zeros((3, 128, 256), np.int32)
    for gi, d in enumerate(PATS):
        idx[gi] = t5_buckets(np.clip(dist, 0, 128) * d)
    return idx, valid


def sample_bias_tables():
    idx = np.zeros((9, 128, 4), np.int32); add = np.full((9, 128, 4), NEG, np.float32)
    def put(ti, p, t, dist, m):
        idx[ti, p, t] = t5_buckets(np.array([dist]))[0]; add[ti, p, t] = np.log(np.float32(m))
    for tp in range(4):
        for p in range(96):
            put(tp, p, tp, 2048 - 16 * p, 1)
    for a in range(4):
        for p in range(128):
            for t in range(4):
                dist = 512 + t - 128 * a - p
                m = int(dist <= 128) + int(dist % 4 == 0 and dist <= 512) + int(dist % 16 == 0)
                if m > 0: put(4 + a, p, t, dist, m)
    for p in range(4):
        for t in range(4):
            dist = t - p
            if dist >= 0: put(8, p, t, dist, 3 if dist == 0 else 1)
    return idx, add


STAGES = ['norm', 'mem', 'kvtm', 'attn', 'sattn', 'cattn', 'conv', 'merge', 'final']


SQ = 'pool'
VARIANT = ''


def build(stop=None):
    nc = bass.Bass("TRN2", target_bir_lowering=False)
    dI = lambda n, s, dt=F32: nc.dram_tensor(n, list(s), dt, kind="ExternalInput").ap()
    dO = lambda n, s: nc.dram_tensor(n, list(s), F32, kind="ExternalOutput").ap()
    xo = dI("xo", [T, 1024]); xh = dI("xh", [T, 1024]); xs = dI("xs", [TS, 1024]); mem = dI("mem", [256, 1024])
    hneg_d = dI("hneg", [128, 1])
    ckw = dI("ckw", [16, 2048, 768]); cvw = dI("cvw", [16, 2048, 768]); scv = dI("scv", [16, 30, 768])
    ckm = dI("ckm", [16, 256, 512]); cvm = dI("cvm", [16, 256, 512])
    w_in = dI("w_in", [1024, 9472]); w_mkv = dI("w_mkv", [1024, 1024])
    cpar = dI("cpar", [34, 768])
    w_pa = dI("w_pa", [768, 1024]); w_pb = dI("w_pb", [768, 1024]); w_pc = dI("w_pc", [512, 1024]); w_o = dI("w_o", [1024, 1024])
    gvec = dI("gvec", [3, 1024])
    pbt = dI("pbt", [6, 128, 6, 256])
    sbg = dI("sbg", [128, 9, 48]); sbm = dI("sbm", [128, 9, 48])
    dmk = dI("dmk", [48, 768]); sel = dI("sel", [48, 4])
    yo = dO("yo", [T, 1024]); ys = dO("ys", [TS, 1024])
    kwo = dO("kwo", [T, 768]); vwo = dO("vwo", [T, 768]); cvo = dO("cvo", [30, 768])
    kmo = dO("kmo", [256, 512]); vmo = dO("vmo", [256, 512])
    kso = dO("kso", [TS, 768]); vso = dO("vso", [TS, 768]); cso = dO("cso", [16, 30, 768])
    Bvso = Buf('vso')

    w_in_v = w_in.rearrange("(k p) c -> p k c", p=128)
    w_mkv_v = w_mkv.rearrange("(k p) c -> p k c", p=128)
    w_pa_v = w_pa.rearrange("(k p) c -> p k c", p=128); w_pb_v = w_pb.rearrange("(k p) c -> p k c", p=128)
    w_pc_v = w_pc.rearrange("(k p) c -> p k c", p=128); w_o_v = w_o.rearrange("(k p) c -> p k c", p=128)

    with ExitStack() as es:
        S = Sched(nc, es)
        def sb(stack, name, shape, dt): return stack.enter_context(nc.sbuf_tensor('t_' + name, list(shape), dt))
        def mk(stack, name, shape, dt, dma=False):
            return Tl(sb(stack, name, shape, dt), name, S.dma_sem('d_' + name) if dma else None)
        def ring(stack, name, shape, dt, n, dma=False):
            return Ring([mk(stack, '%s%d' % (name, i), shape, dt, dma) for i in range(n)])
        pbanks = [Tl(es.enter_context(nc.psum_tensor('pb%d' % i, [128, 512], F32)), 'pb%d' % i) for i in range(8)]
        banks = Ring(pbanks[:6]); banksH = Ring(pbanks[6:]); banks8 = Ring(pbanks)
        dconst = S.dma_sem('d_const'); const_bufs = []
        def cload(q, out_ap, in_ap, tl):
            S.dma(q, out_ap, in_ap, dconst, writes=[tl.b]); const_bufs.append(tl.b)
        def const_done():
            for b in const_bufs: b.w = (dconst['key'], dconst['val'])
            del const_bufs[:]
        dout = {}
        def store(q, out_ap, in_ap, tl, key, writes=()):
            if key not in dout: dout[key] = S.dma_sem('do_' + key)
            return S.dma(q, out_ap, in_ap, dout[key], reads=[tl.b], writes=list(writes))
        def load(q, tl, out_ap, in_ap, reads=()):
            return S.dma(q, out_ap, in_ap, tl.d, reads=list(reads), writes=[tl.b])
        def load_parts(q, tl, parts):
            for i, (o, a) in enumerate(parts):
                S.dma(q, o, a, tl.d, writes=[tl.b] if i == 0 else [])
            if not S.halt: tl.b.w = (tl.d['key'], tl.d['val'])

        ident32 = mk(es, 'ident32', [128, 128], F32); identb = mk(es, 'identb', [128, 128], BF16)
        onesb = mk(es, 'onesb', [128, 128], BF16)
        hneg = mk(es, 'hneg', [128, 1], F32)
        xT = mk(es, 'xT', [128, 8, NT], BF16)
        gaT = mk(es, 'gaT', [128, 6, NT], BF16)
        kTs = mk(es, 'kTs', [128, 6, TS], BF16); qTs = mk(es, 'qTs', [128, 6, TS], BF16); zaSs = mk(es, 'zaSs', [128, 6, TS], F32)
        xh30 = mk(es, 'xh30', [128, 8, 32], BF16)
        kmT = mk(es, 'kmT', [128, 4, 256], BF16); vmp = mk(es, 'vmp', [128, 2, 512], BF16)
        ss_r = ring(es, 'ss', [128, 4], F32, 6); rs_r = ring(es, 'rs', [128, 1], F32, 6)

        S.op('pool', lambda e: e.memset(ident32.t[:], 0.0), writes=[ident32.b])
        S.op('pool', lambda e: e.affine_select(out=ident32.t[:], in_=ident32.t[:], pattern=[[-1, 128]], compare_op=ALU.not_equal,
                                               fill=1.0, base=0, channel_multiplier=1), reads=[ident32.b], writes=[ident32.b])
        S.op('pool', lambda e: e.tensor_copy(out=identb.t[:], in_=ident32.t[:]), reads=[ident32.b], writes=[identb.b])
        S.op('pool', lambda e: e.memset(onesb.t[:], 1.0), writes=[onesb.b])
        cload('sp', hneg.t[:], hneg_d[:], hneg)

        def evac(eng, out_ap, in_ap, func=None, scale=None, bias=None):
            if eng == 'act':
                kw = {}
                if scale is not None: kw['scale'] = scale
                if bias is not None: kw['bias'] = bias
                f = func if func is not None else (AF.Identity if bias is not None else AF.Copy)
                return lambda e: e.activation(out=out_ap, in_=in_ap, func=f, **kw)
            assert func is None and bias is None
            if scale is not None:
                return lambda e: e.tensor_scalar(out=out_ap, in0=in_ap, scalar1=float(scale), scalar2=None, op0=ALU.mult)
            return lambda e: e.tensor_copy(out=out_ap, in_=in_ap)

        def ev_op(out_ap, in_ap, rd, wr, func=None, scale=None, bias=None, eng=None):
            if eng is None:
                eng = S.ev() if func is None and bias is None else 'act'
            S.op(eng, evac(eng, out_ap, in_ap, func, scale, bias), reads=rd, writes=wr)

        def mm_fm(bank, M, n, wt, wcol, xt, c0, KC=8):
            def f(e):
                for k in range(KC):
                    i = e.matmul(bank.t[0:M, 0:n], lhsT=wt.t[:, k, wcol:wcol + M], rhs=xt.t[:, k, c0:c0 + n], start=(k == 0), stop=(k == KC - 1))
                return i
            return f

        def mm_tm(bank, rows, ncol, xt, c0, wt, wcol, KC=8):
            def f(e):
                for k in range(KC):
                    i = e.matmul(bank.t[0:rows, 0:ncol], lhsT=xt.t[:, k, c0:c0 + rows], rhs=wt.t[:, k, wcol:wcol + ncol], start=(k == 0), stop=(k == KC - 1))
                return i
            return f

        def dve(fn, rd, wr): S.op('dve', fn, reads=rd, writes=wr)
        def act(fn, rd, wr): S.op('act', fn, reads=rd, writes=wr)
        def pe(fn, rd, wr): S.op('pe', fn, reads=rd, writes=wr)
        def pool(fn, rd, wr): S.op('pool', fn, reads=rd, writes=wr)

        def rstd_from_ss(rs, ssum_ap, rows, n):
            dve(lambda e: e.tensor_scalar(out=rs.t[:rows], in0=ssum_ap, scalar1=1.0 / n, scalar2=EPS, op0=ALU.mult, op1=ALU.add), [], [rs.b])
            act(lambda e: e.activation(out=rs.t[:rows], in_=rs.t[:rows], func=AF.Sqrt), [rs.b], [rs.b])
            dve(lambda e: e.reciprocal(out=rs.t[:rows], in_=rs.t[:rows]), [rs.b], [rs.b])

        GROUPS = [(g * 512, 512) for g in range(4)] + [(T, TS)]

        try:
            with ExitStack() as sA:
                xTh = mk(sA, 'xTh', [128, 8, T], BF16)
                memT = mk(sA, 'memT', [128, 8, 256], BF16)
                with ExitStack() as p1:
                    xin_r = ring(p1, 'xin', [128, 1024], F32, 9, dma=True); xn_r = ring(p1, 'xn', [128, 1024], BF16, 3); rs4_r = ring(p1, 'rs4', [128, 4], F32, 3)
                    junk_r = ring(p1, 'junk', [128, 1024], F32, 2)
                    gb2 = mk(p1, 'gb2', [128, 2, 1024], F32)
                    for i in range(2):
                        cload('sp', gb2.t[:, i, :], gvec[i].partition_broadcast(128), gb2)
                    const_done()
                    def norm_A(grp):
                        ssq = ss_r.next(); sts = []
                        for j, (src, rows, gi, dst, col0) in enumerate(grp):
                            s = xin_r.next(); junk = junk_r.next()
                            load('sp', s, s.t[:rows, :], src)
                            act(lambda e, s=s, junk=junk, rows=rows: e.activation(out=junk.t[:rows, :], in_=s.t[:rows, :], func=AF.Square), [s.b], [junk.b])
                            dve(lambda e, ssq=ssq, junk=junk, rows=rows, j=j: e.tensor_reduce(out=ssq.t[:rows, j:j + 1], in_=junk.t[:rows, :], axis=AX.X, op=ALU.add), [junk.b], [ssq.b])
                            sts.append((s, rows, gi, dst, col0))
                        return (ssq, sts)
                    def norm_B(st):
                        ssq, sts = st
                        rows = sts[0][1]; n = len(sts); rs = rs4_r.next()
                        dve(lambda e: e.tensor_scalar(out=rs.t[:rows, 0:n], in0=ssq.t[:rows, 0:n], scalar1=1.0 / 1024, scalar2=EPS, op0=ALU.mult, op1=ALU.add),
                            [ssq.b], [rs.b])
                        act(lambda e: e.activation(out=rs.t[:rows, 0:n], in_=rs.t[:rows, 0:n], func=AF.Sqrt), [rs.b], [rs.b])
                        dve(lambda e: e.reciprocal(out=rs.t[:rows, 0:n], in_=rs.t[:rows, 0:n]), [rs.b], [rs.b])
                        for j, (s, rows, gi, dst, col0) in enumerate(sts):
                            xn = xn_r.next()
                            dve(lambda e, xn=xn, s=s, rows=rows, j=j, gi=gi: e.scalar_tensor_tensor(out=xn.t[:rows, :], in0=s.t[:rows, :], scalar=rs.t[:rows, j:j + 1], in1=gb2.t[:rows, gi, :],
                                                                 op0=ALU.mult, op1=ALU.mult), [s.b, rs.b, gb2.b], [xn.b])
                            bank = banks.next()
                            pbf = bank.t[:].bitcast(BF16)
                            def tr(e, pbf=pbf, xn=xn, rows=rows):
                                for k in range(8):
                                    i = e.transpose(out=pbf[:, k * 128:k * 128 + rows], in_=xn.t[:rows, k * 128:(k + 1) * 128], identity=identb.t[:rows, :rows])
                                return i
                            pe(tr, [xn.b, identb.b], [bank.b])
                            ev_op(dst.t[:, :, col0:col0 + rows], pbf.rearrange("p (k t) -> p k t", k=8)[:, :, 0:rows], [bank.b], [dst.b])
                    blocks = [(xh[blk * 128:(blk + 1) * 128, :], 128, 0, xTh, blk * 128) for blk in range(16)]
                    blocks += [(xo[blk * 128:(blk + 1) * 128, :], 128, 0, xT, blk * 128) for blk in range(16)]
                    groups_n = [blocks[i:i + 4] for i in range(0, 32, 4)]
                    groups_n.append([(xs[:, :], 64, 0, xT, T)])
                    groups_n.append([(mem[blk * 128:(blk + 1) * 128, :], 128, 1, memT, blk * 128) for blk in range(2)])
                    npend = []
                    for grp in groups_n:
                        npend.append(norm_A(grp))
                        if len(npend) > 1: norm_B(npend.pop(0))
                    while npend: norm_B(npend.pop(0))
                    dve(lambda e: e.tensor_copy(out=xh30.t[:, :, 0:30], in_=xTh.t[:, :, T - 30:T]), [xTh.b], [xh30.b])
                S.barrier()
                if stop == 'norm': S.halt = True
                with ExitStack() as p2:
                    wm = mk(p2, 'wm', [128, 8, 1024], BF16, dma=True)
                    st_r = ring(p2, 'stm', [128, 512], F32, 2)
                    load_parts('pool', wm, [(wm.t[:, :, hh * 512:(hh + 1) * 512], w_mkv_v[:, :, hh * 512:(hh + 1) * 512]) for hh in range(2)])
                    for h in range(4):
                        bank = banks.next()
                        pe(mm_fm(bank, 128, 256, wm, h * 128, memT, 0), [wm.b, memT.b], [bank.b])
                        ev_op(kmT.t[:, h, :], bank.t[:, 0:256], [bank.b], [kmT.b])
                    if stop == 'mem1': S.halt = True
                    for kv in range(2):
                        for blk in range(2):
                            bank = banks.next(); st = st_r.next()
                            pe(mm_tm(bank, 128, 512, memT, blk * 128, wm, kv * 512), [wm.b, memT.b], [bank.b])
                            ev_op(st.t[:, :], bank.t[:, :], [bank.b], [st.b], eng='act')
                            if kv == 1 and VARIANT != 'A':
                                ev_op(vmp.t[:, blk, :], st.t[:, :], [st.b], [vmp.b], eng='dve')
                            dst = (kmo if kv == 0 else vmo)[blk * 128:(blk + 1) * 128, :]
                            store(SQ, dst, st.t[:, :], st, 'stm%d' % ((st_r.i - 1) % 2))
                S.barrier()
                if stop == 'mem': S.halt = True
                with ExitStack() as p25:
                    wr = ring(p25, 'wr', [128, 8, 768], BF16, 2, dma=True)
                    st32_r = ring(p25, 'st32', [128, 768], F32, 3)
                    for kv in range(2):
                        wt = wr.next()
                        load('pool', wt, wt.t[:], w_in_v[:, :, 768 * (kv + 1):768 * (kv + 2)])
                        for blk in range(17):
                            rows = 128 if blk < 16 else 64
                            st = st32_r.next()
                            for half in range(2):
                                bank = banks.next()
                                pe(mm_tm(bank, rows, 384, xT, blk * 128, wt, half * 384), [wt.b, xT.b], [bank.b])
                                ev_op(st.t[:rows, half * 384:(half + 1) * 384], bank.t[:rows, 0:384], [bank.b], [st.b])
                            if blk < 16:
                                dst = (kwo if kv == 0 else vwo)[blk * 128:(blk + 1) * 128, :]; wrt = ()
                            else:
                                dst = (kso if kv == 0 else vso)[:, :]; wrt = (Bvso,) if kv == 1 else ()
                            store('sp', dst, st.t[:rows, :], st, 'st32_%d' % ((st32_r.i - 1) % 3), writes=wrt)
                S.barrier()
                if stop == 'kvtm': S.halt = True
                with ExitStack() as pA:
                    wq_r = ring(pA, 'wq', [128, 8, 128], BF16, 2, dma=True); wk_r = ring(pA, 'wk', [128, 8, 128], BF16, 2, dma=True)
                    wv_r = ring(pA, 'wv', [128, 8, 128], BF16, 2, dma=True); wz_r = ring(pA, 'wz', [128, 8, 128], BF16, 2, dma=True)
                    pb_r = ring(pA, 'pbt', [128, 6, 256], BF16, 2, dma=True)
                    KVQ = [dict(kT=mk(pA, 'kT%d' % i, [128, 2 * T], BF16), vT=mk(pA, 'vT%d' % i, [128, 2 * T], BF16), qT=mk(pA, 'qT%d' % i, [128, 2, T], BF16)) for i in range(2)]
                    acc = mk(pA, 'acc', [128, 2, T], F32)
                    pt_r = ring(pA, 'pt', [128, 256], BF16, 8)
                    vt_r = ring(pA, 'vt', [128, 256], BF16, 8)
                    R_r = ring(pA, 'Rn', [128, 512], F32, 2); zs_r = ring(pA, 'zs', [128, 512], F32, 2)
                    for v in vt_r.tiles:
                        pool(lambda e, v=v: e.memset(v.t[:, 64:192], 1.0), [], [v.b])
                    hmask = mk(pA, 'hmask', [128, 1], F32)
                    vth_r = ring(pA, 'vth', [128, 256], BF16, 4)
                    dve(lambda e: e.tensor_scalar(out=hmask.t[:, :], in0=hneg.t[:, :], scalar1=0.0, scalar2=None, op0=ALU.is_equal), [hneg.b], [hmask.b])
                    for v in vth_r.tiles:
                        dve(lambda e, v=v: e.tensor_scalar(out=v.t[:, 64:192], in0=onesb.t[:, :], scalar1=hmask.t[:, 0:1], scalar2=None, op0=ALU.mult), [onesb.b, hmask.b], [v.b])
                    for i in range(2):
                        pool(lambda e, q=KVQ[i]['qT']: e.memset(q.t[:, :, :], 0.0), [], [KVQ[i]['qT'].b])
                    def load_w(c):
                        W = dict(wq=wq_r.next(), wk=wk_r.next(), wv=wv_r.next(), wz=wz_r.next(), pb=pb_r.next())
                        load('pool', W['wk'], W['wk'].t[:], w_in_v[:, :, 768 + c * 128:768 + (c + 1) * 128])
                        load('pool', W['wv'], W['wv'].t[:], w_in_v[:, :, 1536 + c * 128:1536 + (c + 1) * 128])
                        load('pool', W['wq'], W['wq'].t[:], w_in_v[:, :, c * 128:(c + 1) * 128])
                        load('pool', W['pb'], W['pb'].t[:], pbt[c])
                        load('pool', W['wz'], W['wz'].t[:], w_in_v[:, :, 2304 + c * 128:2304 + (c + 1) * 128])
                        return W
                    def proj_steps(c, W, bufs):
                        for (wt, dstT) in ((W['wk'], bufs['kT']), (W['wv'], bufs['vT'])):
                            for g in range(8):
                                xt, c0 = (xTh, g * 512) if g < 4 else (xT, (g - 4) * 512)
                                bank = banks8.next()
                                pe(mm_fm(bank, 128, 512, wt, 0, xt, c0), [wt.b, xt.b], [bank.b])
                                ev_op(dstT.t[:, g * 512:(g + 1) * 512], bank.t[:, :], [bank.b], [dstT.b])
                                yield
                        wk = W['wk']; wq = W['wq']; qT = bufs['qT']
                        bank = banks8.next()
                        pe(mm_fm(bank, 128, TS, wk, 0, xT, T), [wk.b, xT.b], [bank.b])
                        ev_op(kTs.t[:, c, :], bank.t[:, 0:TS], [bank.b], [kTs.b])
                        for g in range(4):
                            bank = banks8.next()
                            pe(mm_fm(bank, 128, 512, wq, 0, xT, g * 512), [wq.b, xT.b], [bank.b])
                            ev_op(qT.t[0:64, 0, g * 512:(g + 1) * 512], bank.t[0:64, :], [bank.b], [qT.b], scale=0.125, eng='act')
                            ev_op(qT.t[64:128, 1, g * 512:(g + 1) * 512], bank.t[64:128, :], [bank.b], [qT.b], scale=0.125, eng='act')
                            yield
                        bank = banks8.next()
                        pe(mm_fm(bank, 128, TS, wq, 0, xT, T), [wq.b, xT.b], [bank.b])
                        ev_op(qTs.t[:, c, :], bank.t[:, 0:TS], [bank.b], [qTs.b], scale=0.125)
                        yield
                    Ws = [None] * 6
                    Ws[0] = load_w(0)
                    for _ in proj_steps(0, Ws[0], KVQ[0]): pass
                    for c in range(6):
                        W = Ws[c]; bufs = KVQ[c % 2]
                        kT = bufs['kT']; vT = bufs['vT']; qT = bufs['qT']; wz = W['wz']; pb = W['pb']
                        nxt = None
                        if c < 5:
                            Ws[c + 1] = load_w(c + 1)
                            nxt = proj_steps(c + 1, Ws[c + 1], KVQ[(c + 1) % 2])
                        vcache = {}
                        def get_vt(d, r, Bp, vT=vT, vcache=vcache):
                            key = (d, r, Bp)
                            rg = vth_r if Bp < 16 // d else vt_r
                            if key in vcache and vcache[key][1] > rg.i - len(rg.tiles): return vcache[key][0]
                            vt = rg.next(); vcache[key] = (vt, rg.i, rg)
                            s0 = r + d * 128 * Bp
                            bank = banks8.next(); pbf = bank.t[:].bitcast(BF16)
                            pe(lambda e, pbf=pbf, vT=vT, s0=s0, d=d: e.transpose(out=pbf[:, 0:128], in_=vT.t[:, s0:s0 + d * 127 + 1:d], identity=identb.t[:]), [vT.b, identb.b], [bank.b])
                            ev_op(vt.t[:].rearrange("p (a e) -> p a e", e=64)[:, 0:4:3, :], pbf[:, 0:128].rearrange("p (a e) -> p a e", e=64), [bank.b], [vt.b], eng='act')
                            return vt
                        SK = 4
                        pending = []
                        def S_part(u, kT=kT, qT=qT, pb=pb):
                            bS = banks8.next(); pt = pt_r.next(); u['pt'] = pt
                            def fS(e, bS=bS, k0=u['k0'], qsl=u['qsl'], hl=u['hl'], gi=u['gi'], d=u['d'], pb=pb, kT=kT, qT=qT):
                                for kb in range(2):
                                    e.matmul(bS.t[:, kb * 128:(kb + 1) * 128], lhsT=identb.t[:], rhs=pb.t[:, gi * 2 + hl, kb * 128:(kb + 1) * 128], start=True, stop=False)
                                    i = e.matmul(bS.t[:, kb * 128:(kb + 1) * 128], lhsT=kT.t[:, k0[kb]:k0[kb] + d * 127 + 1:d], rhs=qT.t[:, hl, qsl], start=False, stop=True)
                                return i
                            pe(fS, [identb.b, pb.b, kT.b, qT.b], [bS.b])
                            act(lambda e, bS=bS, pt=pt: e.activation(out=pt.t[:, :], in_=bS.t[:, 0:256], func=AF.Exp), [bS.b], [pt.b])
                        def O_part(u, vcache=vcache):
                            bO = banks8.next(); pt = u['pt']; vts = u['vts']; hl = u['hl']; qsl = u['qsl']
                            for kb in range(2):
                                ent = vcache[u['vkeys'][kb]]; assert ent[0] is vts[kb] and ent[1] > ent[2].i - len(ent[2].tiles)
                            def fO(e, bO=bO, pt=pt, vts=vts, hl=hl):
                                for kb in range(2):
                                    i = e.matmul(bO.t[:, 0:128], lhsT=vts[kb].t[:, hl * 128:(hl + 1) * 128], rhs=pt.t[:, kb * 128:(kb + 1) * 128], start=(kb == 0), stop=(kb == 1))
                                return i
                            pe(fO, [pt.b, vts[0].b, vts[1].b], [bO.b])
                            if u['first']:
                                dve(lambda e, bO=bO, hl=hl, qsl=qsl: e.tensor_copy(out=acc.t[:, hl, qsl], in_=bO.t[:, 0:128]), [bO.b], [acc.b])
                            else:
                                dve(lambda e, bO=bO, hl=hl, qsl=qsl: e.tensor_tensor(out=acc.t[:, hl, qsl], in0=acc.t[:, hl, qsl], in1=bO.t[:, 0:128], op=ALU.add), [bO.b, acc.b], [acc.b])
                        first = True; nu = 0
                        for gi, d in ((2, 16), (1, 4), (0, 1)):
                            B0 = 16 // d
                            for r in range(d):
                                for B in range(B0, 2 * B0):
                                    k0 = [r + d * 128 * (B - 1), r + d * 128 * B]
                                    q0 = r + d * 128 * B - T
                                    qsl = slice(q0, q0 + d * 127 + 1, d)
                                    vts = [get_vt(d, r, B - 1), get_vt(d, r, B)]
                                    for hl in range(2):
                                        u = dict(gi=gi, d=d, k0=k0, qsl=qsl, vts=vts, vkeys=[(d, r, B - 1), (d, r, B)], hl=hl, halo=(B == B0), first=first)
                                        S_part(u); pending.append(u)
                                        if len(pending) > SK: O_part(pending.pop(0))
                                        nu += 1
                                        if nxt is not None and nu % 4 == 0: next(nxt, None)
                            first = False
                        while pending: O_part(pending.pop(0))
                        if nxt is not None:
                            for _ in nxt: pass
                        for g in range(4):
                            sl = slice(g * 512, (g + 1) * 512)
                            bz = banks8.next(); zs = zs_r.next(); R = R_r.next()
                            pe(mm_fm(bz, 128, 512, wz, 0, xT, g * 512), [wz.b, xT.b], [bz.b])
                            act(lambda e, bz=bz, zs=zs: e.activation(out=zs.t[:, :], in_=bz.t[:, :], func=AF.Silu), [bz.b], [zs.b])
                            dve(lambda e, R=R, sl=sl: e.tensor_copy(out=R.t[0:64, :], in_=acc.t[64:128, 0, sl]), [acc.b], [R.b])
                            dve(lambda e, R=R, sl=sl: e.tensor_copy(out=R.t[64:128, :], in_=acc.t[0:64, 1, sl]), [acc.b], [R.b])
                            dve(lambda e, R=R: e.reciprocal(out=R.t[:, :], in_=R.t[:, :]), [R.b], [R.b])
                            dve(lambda e, R=R, zs=zs: e.tensor_tensor(out=R.t[:, :], in0=R.t[:, :], in1=zs.t[:, :], op=ALU.mult), [R.b, zs.b], [R.b])
                            dve(lambda e, R=R, sl=sl, c=c: e.tensor_tensor(out=gaT.t[0:64, c, sl], in0=acc.t[0:64, 0, sl], in1=R.t[0:64, :], op=ALU.mult), [R.b, acc.b], [gaT.b])
                            dve(lambda e, R=R, sl=sl, c=c: e.tensor_tensor(out=gaT.t[64:128, c, sl], in0=acc.t[64:128, 1, sl], in1=R.t[64:128, :], op=ALU.mult), [R.b, acc.b], [gaT.b])
                        bz = banks8.next()
                        pe(mm_fm(bz, 128, TS, wz, 0, xT, T), [wz.b, xT.b], [bz.b])
                        act(lambda e, bz=bz, c=c: e.activation(out=zaSs.t[:, c, :], in_=bz.t[:, 0:TS], func=AF.Silu), [bz.b], [zaSs.b])
                        if stop == 'attn1': S.halt = True
            S.barrier()
            if stop == 'attn': S.halt = True
            with ExitStack() as pB:
                sbt32 = mk(pB, 'sbt32', [128, 9, 48], F32); sbm32 = mk(pB, 'sbm32', [128, 9, 48], F32); sbt = mk(pB, 'sbt', [128, 9, 48], BF16)
                dmk_t = mk(pB, 'dmk', [48, 768], F32); sel_t = mk(pB, 'sel', [48, 4], F32)
                Qbd = mk(pB, 'Qbd', [128, 6, 16, 48], BF16)
                vnew = mk(pB, 'vnew', [4, 16, 769], BF16, dma=True)
                kt_r = ring(pB, 'kt', [128, 768], BF16, 4, dma=True); vt_r = ring(pB, 'svt', [128, 769], BF16, 12, dma=True)
                ktT_r = ring(pB, 'ktT', [128, 6, 128], BF16, 3)
                spt_r = ring(pB, 'spt', [128, 432], BF16, 2)
                Om_r = ring(pB, 'Om', [48, 768], F32, 2); rl_r = ring(pB, 'rl', [48, 1], F32, 2)
                cload('sp', sbt32.t[:], sbg[:], sbt32); cload('sp', sbm32.t[:], sbm[:], sbm32)
                cload('sp', dmk_t.t[:], dmk[:], dmk_t); cload('sp', sel_t.t[:], sel[:], sel_t)
                const_done()
                dve(lambda e: e.tensor_tensor(out=sbt.t[:], in0=sbt32.t[:], in1=sbm32.t[:], op=ALU.add), [sbt32.b, sbm32.b], [sbt.b])
                pool(lambda e: e.memset(Qbd.t[:], 0.0), [], [Qbd.b])
                for c in range(6):
                    for hl in range(2):
                        h = 2 * c + hl
                        dve(lambda e, c=c, hl=hl, h=h: e.tensor_copy(out=Qbd.t[hl * 64:(hl + 1) * 64, c, :, h * 4:(h + 1) * 4],
                                                                      in_=qTs.t[hl * 64:(hl + 1) * 64, c, :].rearrange("p (b t) -> p b t", t=4)), [qTs.b], [Qbd.b])
                for v in vt_r.tiles:
                    pool(lambda e, v=v: e.memset(v.t[:, 768:769], 1.0), [], [v.b])
                pool(lambda e: e.memset(vnew.t[:, :, 768:769], 1.0), [], [vnew.b])
                S.dma('pool', vnew.t[:, :, 0:768], vso.rearrange("(b t) f -> t b f", t=4), vnew.d, reads=[Bvso], writes=[vnew.b])
                for b in range(16):
                    bS = banksH.next(); spt = spt_r.next()
                    vts = []
                    for ti in range(8):
                        kt = kt_r.next(); vt = vt_r.next(); vts.append(vt)
                        if ti < 4:
                            ksrc = ckw[b, ti:2048:16, :]; vsrc = cvw[b, ti:2048:16, :]
                        else:
                            r0 = 1536 + 128 * (ti - 4)
                            ksrc = ckw[b, r0:r0 + 128, :]; vsrc = cvw[b, r0:r0 + 128, :]
                        load('pool', kt, kt.t[:, :], ksrc)
                        load('pool', vt, vt.t[:, 0:768], vsrc)
                        bT = banks.next(); pbf = bT.t[:].bitcast(BF16); ktT = ktT_r.next()
                        def ftr(e, pbf=pbf, kt=kt):
                            for c in range(6):
                                i = e.transpose(out=pbf[:, c * 128:(c + 1) * 128], in_=kt.t[:, c * 128:(c + 1) * 128], identity=identb.t[:])
                            return i
                        pe(ftr, [kt.b, identb.b], [bT.b])
                        ev_op(ktT.t[:, :, :], pbf[:, 0:768].rearrange("p (c k) -> p c k", c=6), [bT.b], [ktT.b])
                        def fS(e, bS=bS, ktT=ktT, ti=ti, b=b):
                            o = bS.t[:, ti * 48:(ti + 1) * 48]
                            e.matmul(o, lhsT=identb.t[:], rhs=sbt.t[:, ti, :], start=True, stop=False)
                            for c in range(6):
                                i = e.matmul(o, lhsT=ktT.t[:, c, :], rhs=Qbd.t[:, c, b, :], start=False, stop=(c == 5))
                            return i
                        pe(fS, [identb.b, sbt.b, ktT.b, Qbd.b], [bS.b])
                    def fS8(e, bS=bS, b=b):
                        o = bS.t[0:4, 384:432]
                        e.matmul(o, lhsT=identb.t[0:4, 0:4], rhs=sbt.t[0:4, 8, :], start=True, stop=False)
                        for c in range(6):
                            i = e.matmul(o, lhsT=kTs.t[:, c, 4 * b:4 * b + 4], rhs=Qbd.t[:, c, b, :], start=False, stop=(c == 5))
                        return i
                    pe(fS8, [identb.b, sbt.b, kTs.b, Qbd.b], [bS.b])
                    act(lambda e, bS=bS, spt=spt: e.activation(out=spt.t[:, 0:384], in_=bS.t[:, 0:384], func=AF.Exp), [bS.b], [spt.b])
                    act(lambda e, bS=bS, spt=spt: e.activation(out=spt.t[0:4, 384:432], in_=bS.t[0:4, 384:432], func=AF.Exp), [bS.b], [spt.b])
                    bO1 = banks.next(); bO2 = banks.next()
                    def fO(e, bO1=bO1, bO2=bO2, spt=spt, vts=vts, b=b):
                        for ti in range(8):
                            e.matmul(bO1.t[0:48, 0:384], lhsT=spt.t[:, ti * 48:(ti + 1) * 48], rhs=vts[ti].t[:, 0:384], start=(ti == 0), stop=False)
                            e.matmul(bO2.t[0:48, 0:385], lhsT=spt.t[:, ti * 48:(ti + 1) * 48], rhs=vts[ti].t[:, 384:769], start=(ti == 0), stop=False)
                        e.matmul(bO1.t[0:48, 0:384], lhsT=spt.t[0:4, 384:432], rhs=vnew.t[0:4, b, 0:384], start=False, stop=True)
                        return e.matmul(bO2.t[0:48, 0:385], lhsT=spt.t[0:4, 384:432], rhs=vnew.t[0:4, b, 384:769], start=False, stop=True)
                    pe(fO, [spt.b, vnew.b] + [v.b for v in vts], [bO1.b, bO2.b])
                    rl = rl_r.next(); Om = Om_r.next()
                    dve(lambda e, rl=rl, bO2=bO2: e.reciprocal(out=rl.t[:, :], in_=bO2.t[0:48, 384:385]), [bO2.b], [rl.b])
                    dve(lambda e, rl=rl, bO1=bO1, Om=Om: e.scalar_tensor_tensor(out=Om.t[:, 0:384], in0=bO1.t[0:48, 0:384], scalar=rl.t[:, 0:1], in1=dmk_t.t[:, 0:384],
                                                                             op0=ALU.mult, op1=ALU.mult), [rl.b, bO1.b, dmk_t.b], [Om.b])
                    dve(lambda e, rl=rl, bO2=bO2, Om=Om: e.scalar_tensor_tensor(out=Om.t[:, 384:768], in0=bO2.t[0:48, 0:384], scalar=rl.t[:, 0:1], in1=dmk_t.t[:, 384:768],
                                                                             op0=ALU.mult, op1=ALU.mult), [rl.b, bO2.b, dmk_t.b], [Om.b])
                    bF = banks.next()
                    def fF(e, bF=bF, Om=Om):
                        for c in range(6):
                            i = e.matmul(bF.t[:, c * 4:(c + 1) * 4], lhsT=Om.t[:, c * 128:(c + 1) * 128], rhs=sel_t.t[:, :], start=True, stop=True)
                        return i
                    pe(fF, [Om.b, sel_t.b], [bF.b])
                    dve(lambda e, bF=bF, b=b: e.tensor_tensor(out=gaT.t[:, :, T + 4 * b:T + 4 * b + 4], in0=bF.t[:, 0:24].rearrange("p (c t) -> p c t", t=4),
                                                              in1=zaSs.t[:, :, 4 * b:4 * b + 4], op=ALU.mult), [bF.b, zaSs.b], [gaT.b])
            S.barrier()
            if stop == 'sattn': S.halt = True
            gcT = mk(es, 'gcT', [128, 4, NT], BF16)
            with ExitStack() as pC:
                wqc = mk(pC, 'wqc', [128, 8, 512], BF16, dma=True); wzc = mk(pC, 'wzc', [128, 8, 512], BF16, dma=True)
                load('pool', wqc, wqc.t[:], w_in_v[:, :, 5376:5888]); load('pool', wzc, wzc.t[:], w_in_v[:, :, 5888:6400])
                qc_r = ring(pC, 'qc', [128, 512], BF16, 3); cpt_r = ring(pC, 'cpt', [128, 2, 512], BF16, 3)
                R_r = ring(pC, 'cR', [128, 512], F32, 2); zs_r = ring(pC, 'czs', [128, 512], F32, 3)
                qcs = mk(pC, 'qcs', [128, 4, TS], BF16); zcs = mk(pC, 'zcs', [128, 4, TS], F32)
                km_r = ring(pC, 'km', [128, 2, 512], BF16, 2, dma=True); vm_r = ring(pC, 'vm', [128, 2, 512], BF16, 2, dma=True)
                kmTs_r = ring(pC, 'kmTs', [128, 4, 256], BF16, 2); spt_r = ring(pC, 'cspt', [128, 32], BF16, 2); sR_r = ring(pC, 'sR', [128, 16], F32, 2)
                SC = 128 ** -0.5
                cpend = []
                def cS_part(h, g):
                    bq = banks8.next(); qc = qc_r.next()
                    pe(mm_fm(bq, 128, 512, wqc, h * 128, xT, g * 512), [wqc.b, xT.b], [bq.b])
                    ev_op(qc.t[:, :], bq.t[:, :], [bq.b], [qc.b], scale=SC)
                    cpt = cpt_r.next(); bs = [banks8.next(), banks8.next()]
                    for m in range(2):
                        pe(lambda e, bm=bs[m], qc=qc, h=h, m=m: e.matmul(bm.t[:, :], lhsT=kmT.t[:, h, m * 128:(m + 1) * 128], rhs=qc.t[:, :], start=True, stop=True),
                           [kmT.b, qc.b], [bs[m].b])
                        act(lambda e, bm=bs[m], cpt=cpt, m=m: e.activation(out=cpt.t[:, m, :], in_=bm.t[:, :], func=AF.Exp), [bs[m].b], [cpt.b])
                    bz = banks8.next(); zs = zs_r.next()
                    pe(mm_fm(bz, 128, 512, wzc, h * 128, xT, g * 512), [wzc.b, xT.b], [bz.b])
                    act(lambda e, bz=bz, zs=zs: e.activation(out=zs.t[:, :], in_=bz.t[:, :], func=AF.Silu), [bz.b], [zs.b])
                    return dict(h=h, g=g, cpt=cpt, zs=zs)
                def cO_part(u):
                    h = u['h']; g = u['g']; cpt = u['cpt']; zs = u['zs']; sl = slice(g * 512, (g + 1) * 512)
                    bO = banks8.next(); bL = banks8.next(); R = R_r.next()
                    def fO(e, bO=bO, bL=bL, cpt=cpt, h=h):
                        for m in range(2):
                            e.matmul(bO.t[:, :], lhsT=vmp.t[:, m, h * 128:(h + 1) * 128], rhs=cpt.t[:, m, :], start=(m == 0), stop=(m == 1))
                        for m in range(2):
                            i = e.matmul(bL.t[:, :], lhsT=onesb.t[:, :], rhs=cpt.t[:, m, :], start=(m == 0), stop=(m == 1))
                        return i
                    pe(fO, [vmp.b, cpt.b, onesb.b], [bO.b, bL.b])
                    dve(lambda e, R=R, bL=bL: e.reciprocal(out=R.t[:, :], in_=bL.t[:, :]), [bL.b], [R.b])
                    dve(lambda e, R=R, zs=zs: e.tensor_tensor(out=R.t[:, :], in0=R.t[:, :], in1=zs.t[:, :], op=ALU.mult), [R.b, zs.b], [R.b])
                    dve(lambda e, R=R, bO=bO, h=h, sl=sl: e.tensor_tensor(out=gcT.t[:, h, sl], in0=bO.t[:, :], in1=R.t[:, :], op=ALU.mult), [R.b, bO.b], [gcT.b])
                for h in range(4):
                    for g in range(4):
                        cpend.append(cS_part(h, g))
                        if len(cpend) > 1: cO_part(cpend.pop(0))
                while cpend: cO_part(cpend.pop(0))
                for h in range(4):
                    bq = banks.next()
                    pe(mm_fm(bq, 128, TS, wqc, h * 128, xT, T), [wqc.b, xT.b], [bq.b])
                    ev_op(qcs.t[:, h, :], bq.t[:, 0:TS], [bq.b], [qcs.b], scale=SC)
                    bz = banks.next()
                    pe(mm_fm(bz, 128, TS, wzc, h * 128, xT, T), [wzc.b, xT.b], [bz.b])
                    act(lambda e, bz=bz, h=h: e.activation(out=zcs.t[:, h, :], in_=bz.t[:, 0:TS], func=AF.Silu), [bz.b], [zcs.b])
                for b in range(16):
                    km = km_r.next(); vm = vm_r.next(); kmTs = kmTs_r.next(); spt = spt_r.next(); sR = sR_r.next()
                    load('pool', km, km.t[:], ckm[b].rearrange("(m p) f -> p m f", p=128))
                    load('pool', vm, vm.t[:], cvm[b].rearrange("(m p) f -> p m f", p=128))
                    bT = banks.next(); pbf = bT.t[:].bitcast(BF16)
                    def ftr(e, pbf=pbf, km=km):
                        for h in range(4):
                            for m in range(2):
                                i = e.transpose(out=pbf[:, h * 256 + m * 128:h * 256 + (m + 1) * 128], in_=km.t[:, m, h * 128:(h + 1) * 128], identity=identb.t[:])
                        return i
                    pe(ftr, [km.b, identb.b], [bT.b])
                    ev_op(kmTs.t[:, :, :], pbf[:, 0:1024].rearrange("p (h k) -> p h k", h=4), [bT.b], [kmTs.b])
                    bS = banks.next()
                    def fS(e, bS=bS, kmTs=kmTs, b=b):
                        for m in range(2):
                            for h in range(4):
                                i = e.matmul(bS.t[:, m * 16 + h * 4:m * 16 + h * 4 + 4], lhsT=kmTs.t[:, h, m * 128:(m + 1) * 128], rhs=qcs.t[:, h, 4 * b:4 * b + 4], start=True, stop=True)
                        return i
                    pe(fS, [kmTs.b, qcs.b], [bS.b])
                    act(lambda e, bS=bS, spt=spt: e.activation(out=spt.t[:, :], in_=bS.t[:, 0:32], func=AF.Exp), [bS.b], [spt.b])
                    bO = banks.next()
                    def fO(e, bO=bO, spt=spt, vm=vm):
                        for h in range(4):
                            for m in range(2):
                                e.matmul(bO.t[:, h * 4:h * 4 + 4], lhsT=vm.t[:, m, h * 128:(h + 1) * 128], rhs=spt.t[:, m * 16 + h * 4:m * 16 + h * 4 + 4], start=(m == 0), stop=(m == 1))
                        for m in range(2):
                            i = e.matmul(bO.t[:, 16:32], lhsT=onesb.t[:, :], rhs=spt.t[:, m * 16:(m + 1) * 16], start=(m == 0), stop=(m == 1))
                        return i
                    pe(fO, [spt.b, vm.b, onesb.b], [bO.b])
                    dve(lambda e, sR=sR, bO=bO: e.reciprocal(out=sR.t[:, :], in_=bO.t[:, 16:32]), [bO.b], [sR.b])
                    dve(lambda e, sR=sR, b=b: e.tensor_tensor(out=sR.t[:, :].rearrange("p (h t) -> p h t", t=4), in0=sR.t[:, :].rearrange("p (h t) -> p h t", t=4),
                                                              in1=zcs.t[:, :, 4 * b:4 * b + 4], op=ALU.mult), [sR.b, zcs.b], [sR.b])
                    dve(lambda e, sR=sR, bO=bO, b=b: e.tensor_tensor(out=gcT.t[:, :, T + 4 * b:T + 4 * b + 4], in0=bO.t[:, 0:16].rearrange("p (h t) -> p h t", t=4),
                                                                     in1=sR.t[:, :].rearrange("p (h t) -> p h t", t=4), op=ALU.mult), [sR.b, bO.b], [gcT.b])
            S.barrier()
            if stop == 'cattn': S.halt = True
            gbT = mk(es, 'gbT', [128, 6, NT], BF16)
            with ExitStack() as pD:
                convb = mk(pD, 'convb', [128, 6, NT], BF16)
                cwT = mk(pD, 'cwT', [128, 6, 34], F32)
                with ExitStack() as pD1:
                    cw_tm = mk(pD1, 'cw_tm', [34, 768], F32)
                    hs = [mk(pD1, 'hs%d' % i, [120, 768], F32) for i in range(4)]
                    wu1_r = ring(pD1, 'wu1', [128, 8, 128], BF16, 2, dma=True); wu2_r = ring(pD1, 'wu2', [128, 8, 128], BF16, 2, dma=True)
                    G_r = ring(pD1, 'G', [128, 30 + T], BF16, 2); GS_r = ring(pD1, 'GS', [128, 16, 34], F32, 2)
                    GSb_r = ring(pD1, 'GSb', [128, 16, 34], BF16, 2); Gl_r = ring(pD1, 'Gl', [128, 32], F32, 2); Dg_r = ring(pD1, 'Dg', [128, 31, 128], BF16, 2)
                    sg_r = ring(pD1, 'sg', [128, 512], F32, 2); gn_r = ring(pD1, 'gn', [128, 64], F32, 2)
                    cvst = mk(pD1, 'cvst', [30, 768], F32); csst = mk(pD1, 'csst', [64, 768], F32)
                    cload('sp', cw_tm.t[:], cpar[:], cw_tm)
                    for i in range(4):
                        cload('sp', hs[i].t[:], scv[4 * i:4 * i + 4].rearrange("b r f -> (b r) f"), hs[i])
                    const_done()
                    bank = banks.next()
                    def ftr(e, bank=bank):
                        for c in range(6):
                            i = e.transpose(out=bank.t[:, c * 34:(c + 1) * 34], in_=cw_tm.t[0:34, c * 128:(c + 1) * 128], identity=ident32.t[0:34, 0:34])
                        return i
                    pe(ftr, [cw_tm.b, ident32.b], [bank.b])
                    ev_op(cwT.t[:, :, :], bank.t[:, 0:204].rearrange("p (c j) -> p c j", c=6), [bank.b], [cwT.b], eng='dve')
                    for b in range(16):
                        i, bb = b // 4, b % 4
                        store('sp', cso[b, 0:26, :], hs[i].t[bb * 30 + 4:bb * 30 + 30, :], hs[i], 'cso_old')
                    for c in range(6):
                        wu1 = wu1_r.next(); wu2 = wu2_r.next(); G = G_r.next(); GS = GS_r.next(); GSb = GSb_r.next(); Gl = Gl_r.next(); Dg = Dg_r.next()
                        for j in range(31):
                            dve(lambda e, Dg=Dg, c=c, j=j: e.tensor_scalar(out=Dg.t[:, j, :], in0=identb.t[:, :], scalar1=cwT.t[:, c, j:j + 1], scalar2=None, op0=ALU.mult), [identb.b, cwT.b], [Dg.b])
                        load('pool', wu1, wu1.t[:], w_in_v[:, :, 3072 + c * 128:3072 + (c + 1) * 128])
                        load('pool', wu2, wu2.t[:], w_in_v[:, :, 3840 + c * 128:3840 + (c + 1) * 128])
                        bank = banks.next()
                        def fh(e, bank=bank, c=c):
                            for i in range(4):
                                r = e.transpose(out=bank.t[:, i * 120:(i + 1) * 120], in_=hs[i].t[0:120, c * 128:(c + 1) * 128], identity=ident32.t[0:120, 0:120])
                            return r
                        pe(fh, [hs[0].b, hs[1].b, hs[2].b, hs[3].b, ident32.b], [bank.b])
                        ev_op(GS.t[:, :, 0:30], bank.t[:, 0:480].rearrange("p (b r) -> p b r", r=30), [bank.b], [GS.b], eng='dve')
                        glist = [(xh30, 0, 30, G.t[:, 0:30])] + [(xT, g * 512, 512, G.t[:, 30 + g * 512:30 + (g + 1) * 512]) for g in range(4)] + [(xT, T, TS, None)]
                        for (xt, c0, n, dst) in glist:
                            b1 = banks.next(); b2 = banks.next(); sg = sg_r.next()
                            pe(mm_fm(b1, 128, n, wu1, 0, xt, c0), [wu1.b, xt.b], [b1.b])
                            pe(mm_fm(b2, 128, n, wu2, 0, xt, c0), [wu2.b, xt.b], [b2.b])
                            act(lambda e, b2=b2, sg=sg, n=n: e.activation(out=sg.t[:, 0:n], in_=b2.t[:, 0:n], func=AF.Sigmoid), [b2.b], [sg.b])
                            if dst is not None:
                                dve(lambda e, b1=b1, sg=sg, n=n, dst=dst: e.tensor_tensor(out=dst, in0=b1.t[:, 0:n], in1=sg.t[:, 0:n], op=ALU.mult), [b1.b, sg.b], [G.b])
                                if c0 == 3 * 512 and n == 512:
                                    dve(lambda e, b1=b1, sg=sg, Gl=Gl: e.tensor_tensor(out=Gl.t[:, 0:30], in0=b1.t[:, 482:512], in1=sg.t[:, 482:512], op=ALU.mult), [b1.b, sg.b], [Gl.b])
                            else:
                                dve(lambda e, b1=b1, sg=sg, GS=GS: e.tensor_tensor(out=GS.t[:, :, 30:34], in0=b1.t[:, 0:TS].rearrange("p (b t) -> p b t", t=4),
                                                                                     in1=sg.t[:, 0:TS].rearrange("p (b t) -> p b t", t=4), op=ALU.mult), [b1.b, sg.b], [GS.b])
                        dve(lambda e, GS=GS, GSb=GSb: e.tensor_copy(out=GSb.t[:, :, :], in_=GS.t[:, :, :]), [GS.b], [GSb.b])
                        for g in range(4):
                            bank = banks.next()
                            def fc(e, bank=bank, Dg=Dg, G=G, g=g):
                                for j in range(31):
                                    i = e.matmul(bank.t[:, :], lhsT=Dg.t[:, j, :], rhs=G.t[:, g * 512 + j:g * 512 + j + 512], start=(j == 0), stop=(j == 30))
                                return i
                            pe(fc, [Dg.b, G.b], [bank.b])
                            act(lambda e, bank=bank, c=c, g=g: e.activation(out=convb.t[:, c, g * 512:(g + 1) * 512], in_=bank.t[:, :], func=AF.Identity, bias=cwT.t[:, c, 31:32]), [bank.b, cwT.b], [convb.b])
                        bank = banks.next()
                        def fcs(e, bank=bank, Dg=Dg, GSb=GSb):
                            for j in range(31):
                                i = e.matmul(bank.t[:, 0:64].rearrange("p (b t) -> p b t", t=4), lhsT=Dg.t[:, j, :], rhs=GSb.t[:, :, j:j + 4], start=(j == 0), stop=(j == 30))
                            return i
                        pe(fcs, [Dg.b, GSb.b], [bank.b])
                        act(lambda e, bank=bank, c=c: e.activation(out=convb.t[:, c, T:NT], in_=bank.t[:, 0:64], func=AF.Identity, bias=cwT.t[:, c, 31:32]), [bank.b, cwT.b], [convb.b])
                        bank = banks.next(); gn = gn_r.next()
                        pe(lambda e, bank=bank, Gl=Gl: e.transpose(out=bank.t[0:30, 0:128], in_=Gl.t[:, 0:30], identity=ident32.t[:, :]), [Gl.b, ident32.b], [bank.b])
                        ev_op(cvst.t[:, c * 128:(c + 1) * 128], bank.t[0:30, 0:128], [bank.b], [cvst.b], eng='act')
                        dve(lambda e, gn=gn, GS=GS: e.tensor_copy(out=gn.t[:, :].rearrange("p (b t) -> p b t", t=4), in_=GS.t[:, :, 30:34]), [GS.b], [gn.b])
                        bank = banks.next()
                        pe(lambda e, bank=bank, gn=gn: e.transpose(out=bank.t[0:64, 0:128], in_=gn.t[:, :], identity=ident32.t[:, :]), [gn.b, ident32.b], [bank.b])
                        ev_op(csst.t[:, c * 128:(c + 1) * 128], bank.t[0:64, 0:128], [bank.b], [csst.b], eng='act')
                    store('sp', cvo[:, :], cvst.t[:, :], cvst, 'cvo')
                    for b in range(16):
                        store('sp', cso[b, 26:30, :], csst.t[4 * b:4 * b + 4, :], csst, 'cso_new')
                S.barrier()
                wzb = mk(pD, 'wzb', [128, 8, 768], BF16, dma=True)
                load('pool', wzb, wzb.t[:], w_in_v[:, :, 4608:5376])
                mean_r = ring(pD, 'mean', [128, 512], F32, 2); rstd_r = ring(pD, 'rstd', [128, 512], F32, 2); msq_r = ring(pD, 'msq', [128, 512], F32, 2)
                sq_r = ring(pD, 'sq', [128, 512], BF16, 3); t_r = ring(pD, 'lt', [128, 512], F32, 3); zs_r = ring(pD, 'bzs', [128, 512], F32, 2)
                for (c0, n) in GROUPS:
                    bsum = banksH.next(); bsq = banksH.next()
                    def fsum(e, bsum=bsum, c0=c0, n=n):
                        for c in range(6):
                            i = e.matmul(bsum.t[:, 0:n], lhsT=onesb.t[:, :], rhs=convb.t[:, c, c0:c0 + n], start=(c == 0), stop=(c == 5))
                        return i
                    pe(fsum, [onesb.b, convb.b], [bsum.b])
                    for c in range(6):
                        sq = sq_r.next()
                        act(lambda e, sq=sq, c=c, c0=c0, n=n: e.activation(out=sq.t[:, 0:n], in_=convb.t[:, c, c0:c0 + n], func=AF.Square), [convb.b], [sq.b])
                        pe(lambda e, bsq=bsq, sq=sq, c=c, n=n: e.matmul(bsq.t[:, 0:n], lhsT=onesb.t[:, :], rhs=sq.t[:, 0:n], start=(c == 0), stop=(c == 5)), [onesb.b, sq.b], [bsq.b])
                    mean = mean_r.next(); rstd = rstd_r.next(); msq = msq_r.next()
                    dve(lambda e, mean=mean, bsum=bsum, n=n: e.tensor_scalar(out=mean.t[:, 0:n], in0=bsum.t[:, 0:n], scalar1=1.0 / 768, scalar2=None, op0=ALU.mult), [bsum.b], [mean.b])
                    dve(lambda e, mean=mean, msq=msq, n=n: e.tensor_tensor(out=msq.t[:, 0:n], in0=mean.t[:, 0:n], in1=mean.t[:, 0:n], op=ALU.mult), [mean.b], [msq.b])
                    dve(lambda e, rstd=rstd, bsq=bsq, msq=msq, n=n: e.scalar_tensor_tensor(out=rstd.t[:, 0:n], in0=bsq.t[:, 0:n], scalar=1.0 / 768, in1=msq.t[:, 0:n], op0=ALU.mult, op1=ALU.subtract),
                        [bsq.b, msq.b], [rstd.b])
                    dve(lambda e, rstd=rstd, n=n: e.tensor_scalar(out=rstd.t[:, 0:n], in0=rstd.t[:, 0:n], scalar1=EPS, scalar2=None, op0=ALU.add), [rstd.b], [rstd.b])
                    act(lambda e, rstd=rstd, n=n: e.activation(out=rstd.t[:, 0:n], in_=rstd.t[:, 0:n], func=AF.Sqrt), [rstd.b], [rstd.b])
                    dve(lambda e, rstd=rstd, n=n: e.reciprocal(out=rstd.t[:, 0:n], in_=rstd.t[:, 0:n]), [rstd.b], [rstd.b])
                    for c in range(6):
                        bz = banks.next(); zs = zs_r.next(); t = t_r.next()
                        pe(mm_fm(bz, 128, n, wzb, c * 128, xT, c0), [wzb.b, xT.b], [bz.b])
                        act(lambda e, bz=bz, zs=zs, n=n: e.activation(out=zs.t[:, 0:n], in_=bz.t[:, 0:n], func=AF.Silu), [bz.b], [zs.b])
                        dve(lambda e, t=t, c=c, c0=c0, n=n, mean=mean: e.tensor_tensor(out=t.t[:, 0:n], in0=convb.t[:, c, c0:c0 + n], in1=mean.t[:, 0:n], op=ALU.subtract), [convb.b, mean.b], [t.b])
                        dve(lambda e, t=t, n=n, rstd=rstd: e.tensor_tensor(out=t.t[:, 0:n], in0=t.t[:, 0:n], in1=rstd.t[:, 0:n], op=ALU.mult), [t.b, rstd.b], [t.b])
                        dve(lambda e, t=t, n=n, c=c: e.tensor_scalar(out=t.t[:, 0:n], in0=t.t[:, 0:n], scalar1=cwT.t[:, c, 32:33], scalar2=cwT.t[:, c, 33:34], op0=ALU.mult, op1=ALU.add), [t.b, cwT.b], [t.b])
                        act(lambda e, t=t, n=n: e.activation(out=t.t[:, 0:n], in_=t.t[:, 0:n], func=AF.Silu), [t.b], [t.b])
                        dve(lambda e, t=t, n=n, zs=zs, c=c, c0=c0: e.tensor_tensor(out=gbT.t[:, c, c0:c0 + n], in0=t.t[:, 0:n], in1=zs.t[:, 0:n], op=ALU.mult), [t.b, zs.b], [gbT.b])
            S.barrier()
            if stop == 'conv': S.halt = True
            mixT = mk(es, 'mixT', [128, 8, NT], BF16)
            wo = mk(es, 'wo', [128, 8, 1024], BF16, dma=True)
            load_parts('pool', wo, [(wo.t[:, :, hh * 512:(hh + 1) * 512], w_o_v[:, :, hh * 512:(hh + 1) * 512]) for hh in range(2)])
            with ExitStack() as pE:
                wg_r = [ring(pE, 'wg%d' % i, [128, 8, 128], BF16, 2, dma=True) for i in range(3)]
                wpa_r = ring(pE, 'wpa', [128, 6, 128], BF16, 2, dma=True); wpb_r = ring(pE, 'wpb', [128, 6, 128], BF16, 2, dma=True)
                wpc_r = ring(pE, 'wpc', [128, 4, 128], BF16, 2, dma=True)
                sg_r = ring(pE, 'msg', [128, 512], F32, 6); t_r = ring(pE, 'mt', [128, 512], F32, 4)
                for k in range(8):
                    ks = slice(k * 128, (k + 1) * 128)
                    wg = [r.next() for r in wg_r]; wpa = wpa_r.next(); wpb = wpb_r.next(); wpc = wpc_r.next()
                    for i in range(3):
                        load('pool', wg[i], wg[i].t[:], w_in_v[:, :, 6400 + i * 1024 + k * 128:6400 + i * 1024 + (k + 1) * 128])
                    load('pool', wpa, wpa.t[:], w_pa_v[:, :, ks]); load('pool', wpb, wpb.t[:], w_pb_v[:, :, ks]); load('pool', wpc, wpc.t[:], w_pc_v[:, :, ks])
                    for (c0, n) in GROUPS:
                        bP = [banks.next() for _ in range(3)]; bG = [banks.next() for _ in range(3)]
                        pe(mm_fm(bP[0], 128, n, wpa, 0, gaT, c0, KC=6), [wpa.b, gaT.b], [bP[0].b])
                        pe(mm_fm(bP[1], 128, n, wpb, 0, gbT, c0, KC=6), [wpb.b, gbT.b], [bP[1].b])
                        pe(mm_fm(bP[2], 128, n, wpc, 0, gcT, c0, KC=4), [wpc.b, gcT.b], [bP[2].b])
                        sgs = []
                        for i in range(3):
                            pe(mm_fm(bG[i], 128, n, wg[i], 0, xT, c0), [wg[i].b, xT.b], [bG[i].b])
                            sg = sg_r.next(); sgs.append(sg)
                            act(lambda e, bg=bG[i], sg=sg, n=n: e.activation(out=sg.t[:, 0:n], in_=bg.t[:, 0:n], func=AF.Sigmoid), [bG[i].b], [sg.b])
                        t = t_r.next(); t2 = t_r.next()
                        dve(lambda e, t=t, sg=sgs[0], bp=bP[0], n=n: e.tensor_tensor(out=t.t[:, 0:n], in0=sg.t[:, 0:n], in1=bp.t[:, 0:n], op=ALU.mult), [sgs[0].b, bP[0].b], [t.b])
                        dve(lambda e, t2=t2, sg=sgs[1], bp=bP[1], n=n: e.tensor_tensor(out=t2.t[:, 0:n], in0=sg.t[:, 0:n], in1=bp.t[:, 0:n], op=ALU.mult), [sgs[1].b, bP[1].b], [t2.b])
                        dve(lambda e, t=t, t2=t2, n=n: e.tensor_tensor(out=t.t[:, 0:n], in0=t.t[:, 0:n], in1=t2.t[:, 0:n], op=ALU.add), [t.b, t2.b], [t.b])
                        dve(lambda e, t2=t2, sg=sgs[2], bp=bP[2], n=n: e.tensor_tensor(out=t2.t[:, 0:n], in0=sg.t[:, 0:n], in1=bp.t[:, 0:n], op=ALU.mult), [sgs[2].b, bP[2].b], [t2.b])
                        dve(lambda e, t=t, t2=t2, n=n, k=k, c0=c0: e.tensor_tensor(out=mixT.t[:, k, c0:c0 + n], in0=t.t[:, 0:n], in1=t2.t[:, 0:n], op=ALU.add), [t.b, t2.b], [mixT.b])
            S.barrier()
            if stop == 'merge': S.halt = True
            with ExitStack() as pF:
                gpost = mk(pF, 'gpost', [128, 1024], F32)
                xin_r = ring(pF, 'xin2', [128, 1024], F32, 4, dma=True); y_r = ring(pF, 'yout', [128, 1024], F32, 2)
                junk_r = ring(pF, 'junk2', [128, 1024], F32, 2)
                cload('sp', gpost.t[:], gvec[2].partition_broadcast(128), gpost)
                const_done()
                def fin_A(blk):
                    rows = 128 if blk < 16 else 64
                    xi = xin_r.next(); ssq = ss_r.next(); junk = junk_r.next()
                    load('sp', xi, xi.t[:rows, :], xo[blk * 128:(blk + 1) * 128, :] if blk < 16 else xs[:, :])
                    bb = [banks.next(), banks.next()]
                    for half in range(2):
                        pe(mm_tm(bb[half], rows, 512, mixT, blk * 128, wo, half * 512), [mixT.b, wo.b], [bb[half].b])
                        act(lambda e, bk=bb[half], half=half, rows=rows, junk=junk: e.activation(out=junk.t[:rows, half * 512:(half + 1) * 512], in_=bk.t[:rows, :], func=AF.Square), [bb[half].b], [junk.b])
                    dve(lambda e, ssq=ssq, rows=rows, junk=junk: e.tensor_reduce(out=ssq.t[:rows, 0:1], in_=junk.t[:rows, :], axis=AX.X, op=ALU.add), [junk.b], [ssq.b])
                    return (blk, rows, xi, ssq, bb)
                def fin_B(st):
                    blk, rows, xi, ssq, bb = st
                    y = y_r.next(); rs = rs_r.next()
                    dve(lambda e, ssq=ssq, rs=rs, rows=rows: e.tensor_scalar(out=rs.t[:rows], in0=ssq.t[:rows, 0:1], scalar1=1.0 / 1024, scalar2=EPS, op0=ALU.mult, op1=ALU.add), [ssq.b], [rs.b])
                    act(lambda e, rs=rs, rows=rows: e.activation(out=rs.t[:rows], in_=rs.t[:rows], func=AF.Sqrt), [rs.b], [rs.b])
                    dve(lambda e, rs=rs, rows=rows: e.reciprocal(out=rs.t[:rows], in_=rs.t[:rows]), [rs.b], [rs.b])
                    for half in range(2):
                        hs_ = slice(half * 512, (half + 1) * 512)
                        dve(lambda e, y=y, bk=bb[half], rs=rs, rows=rows, hs_=hs_: e.scalar_tensor_tensor(out=y.t[:rows, hs_], in0=bk.t[:rows, :], scalar=rs.t[:rows, 0:1], in1=gpost.t[:rows, hs_],
                                                                                                       op0=ALU.mult, op1=ALU.mult), [bb[half].b, rs.b, gpost.b], [y.b])
                    dve(lambda e, y=y, xi=xi, rows=rows: e.tensor_tensor(out=y.t[:rows, :], in0=y.t[:rows, :], in1=xi.t[:rows, :], op=ALU.add), [y.b, xi.b], [y.b])
                    store('sp', yo[blk * 128:(blk + 1) * 128, :] if blk < 16 else ys[:, :], y.t[:rows, :], y, 'y%d' % ((y_r.i - 1) % 2))
                fpend = []
                for blk in range(17):
                    fpend.append(fin_A(blk))
                    if len(fpend) > 1: fin_B(fpend.pop(0))
                while fpend: fin_B(fpend.pop(0))
        except StopBuild:
            pass
        S.emit()
    return nc


def host_inputs(inp):
    f = lambda a: np.ascontiguousarray(np.asarray(a, dtype=np.float32))
    rb = f(inp['rel_bias'])
    pidx, pvalid = prompt_bias_tables()
    pbt = np.zeros((6, 128, 6, 256), np.float32)
    for c in range(6):
        for gi in range(3):
            for hl in range(2):
                pbt[c, :, gi * 2 + hl, :] = np.where(pvalid, rb[pidx[gi], 2 * c + hl], np.float32(NEG))
    sidx, sadd = sample_bias_tables()
    sbg = np.zeros((128, 9, 48), np.float32); sbm = np.zeros((128, 9, 48), np.float32)
    for ti in range(9):
        for h in range(12):
            sbg[:, ti, h * 4:(h + 1) * 4] = np.where(sadd[ti] > -1e29, rb[sidx[ti], h], np.float32(0.0))
            sbm[:, ti, h * 4:(h + 1) * 4] = sadd[ti]
    dmk = np.zeros((48, 12, 64), np.float32)
    for h in range(12): dmk[h * 4:(h + 1) * 4, h, :] = 1.0
    sel = np.zeros((48, 4), np.float32)
    for h in range(12):
        for t in range(4): sel[h * 4 + t, t] = 1.0
    shared = {
        'w_in': f(inp['w_in'][0]), 'w_mkv': f(inp['w_mem_kv'][0]),
        'cpar': f(np.concatenate([inp['conv_w'][0], inp['conv_b'], inp['ln_g'], inp['ln_b']], axis=0)),
        'w_pa': f(inp['w_proj_a'][0]), 'w_pb': f(inp['w_proj_b'][0]), 'w_pc': f(inp['w_proj_c'][0]), 'w_o': f(inp['w_out'][0]),
        'gvec': f(np.concatenate([inp['g_pre'], inp['g_mem'], inp['g_post']], axis=0)),
        'pbt': pbt, 'sbg': sbg, 'sbm': sbm, 'dmk': dmk.reshape(48, 768), 'sel': sel,
    }
    xp = np.asarray(inp['x_prompt']); xsm = np.asarray(inp['x_sample'])
    maps = []
    for c in range(8):
        b, hf = c // 2, c % 2
        m = dict(shared)
        m['xo'] = f(xp[b, hf * T:(hf + 1) * T])
        m['xh'] = f(xp[b, 0:T]) if hf == 1 else np.zeros((T, 1024), np.float32)
        m['xs'] = f(xsm[16 * c:16 * c + 16].reshape(TS, 1024))
        m['mem'] = f(inp['mem_prompt'][b])
        m['hneg'] = np.full((128, 1), 0.0 if hf == 1 else NEG, np.float32)
        m['ckw'] = f(np.asarray(inp['cache_k_win'])[0, 16 * c:16 * c + 16].reshape(16, 2048, 768))
        m['cvw'] = f(np.asarray(inp['cache_v_win'])[0, 16 * c:16 * c + 16].reshape(16, 2048, 768))
        m['scv'] = f(np.asarray(inp['state_conv'])[0, 16 * c:16 * c + 16])
        m['ckm'] = f(np.asarray(inp['cache_k_mem'])[0, 16 * c:16 * c + 16].reshape(16, 256, 512))
        m['cvm'] = f(np.asarray(inp['cache_v_mem'])[0, 16 * c:16 * c + 16].reshape(16, 256, 512))
        maps.append(m)
    return maps


_NC = None


def kernel(**inp):
    global _NC
    if _NC is None:
        _NC = build()
    maps = host_inputs(inp)
    res = run_bass_kernel_spmd(_NC, maps, core_ids=list(range(8)))
    R = res.results
    y_p = np.stack([np.concatenate([R[2 * b]['yo'], R[2 * b + 1]['yo']], axis=0) for b in range(4)])
    y_s = np.concatenate([R[c]['ys'].reshape(16, 4, 1024) for c in range(8)], axis=0)
    kwp = np.stack([R[2 * b + 1]['kwo'].reshape(T, 12, 64) for b in range(4)])[None]
    vwp = np.stack([R[2 * b + 1]['vwo'].reshape(T, 12, 64) for b in range(4)])[None]
    cvp = np.stack([R[2 * b + 1]['cvo'] for b in range(4)])[None]
    kmp = np.stack([R[2 * b]['kmo'].reshape(256, 4, 128) for b in range(4)])[None]
    vmp = np.stack([R[2 * b]['vmo'].reshape(256, 4, 128) for b in range(4)])[None]
    kws = np.concatenate([R[c]['kso'].reshape(16, 4, 12, 64) for c in range(8)], axis=0)[None]
    vws = np.concatenate([R[c]['vso'].reshape(16, 4, 12, 64) for c in range(8)], axis=0)[None]
    cvs = np.concatenate([R[c]['cso'] for c in range(8)], axis=0)[None]
    return tuple(np.ascontiguousarray(a, dtype=np.float32) for a in (y_p, y_s, kwp, vwp, cvp, kmp, vmp, kws, vws, cvs))
```
